# Optimizing a Trainium2 kernel written in Bass

```python
import math
import jax, jax.numpy as jnp
from jax import lax
import numpy as np

D_MODEL = 1024
BATCH = 32
SEQ = 2048
DEPTH = 1

MEM_LEN = 256
EPS = 1e-6
D_FF = 2816
FFN_RES_WEIGHT = 0.5
CONV_A_WIDTH = D_MODEL
CONV_A_K = 3
SSM_D_INNER = 2 * D_MODEL
SSM_HEAD_DIM = 64
SSM_HEADS = SSM_D_INNER // SSM_HEAD_DIM
SSM_GROUPS = 4
SSM_STATE = 128
SSM_CONV_K = 4
SSM_CHUNK = 128
SSM_CONV_CH = SSM_D_INNER + 2 * SSM_GROUPS * SSM_STATE
XATTN_HEADS = 4
XATTN_HEAD_DIM = D_MODEL // XATTN_HEADS
XATTN_SCALE = 1.0 / math.sqrt(XATTN_HEAD_DIM)
IN_SIZES = (CONV_A_WIDTH, CONV_A_WIDTH, CONV_A_WIDTH,
            SSM_D_INNER, SSM_CONV_CH, SSM_HEADS,
            D_MODEL, D_MODEL)
D_IN_PROJ = sum(IN_SIZES)
IN_SPLITS = tuple(int(v) for v in np.cumsum(IN_SIZES)[:-1])

kernel_name = "hybrid_shortconv_ssd_gated_macaron"


def rmsnorm(x, g):
    xf = x.astype(jnp.float32)
    y = xf * lax.rsqrt(jnp.mean(xf * xf, axis=-1, keepdims=True) + EPS)
    return (y * g.astype(jnp.float32)).astype(x.dtype)


def swiglu(u, w_gate_up, w_down):
    gate, up = jnp.split(u @ w_gate_up, 2, axis=-1)
    return (jax.nn.silu(gate) * up) @ w_down


def causal_dwconv(x, w):
    k, c = w.shape
    return lax.conv_general_dilated(
        x, w[:, None, :].astype(x.dtype), window_strides=(1,),
        padding=[(k - 1, 0)], dimension_numbers=('NWC', 'WIO', 'NWC'),
        feature_group_count=c)


def short_conv_branch(b_gate, c_gate, v, conv_w, w_out):
    return (b_gate * causal_dwconv(c_gate * v, conv_w)) @ w_out


def ssd_chunked(xh, dt, a, bm, cm):
    b, s, h, p = xh.shape
    g, n = bm.shape[-2:]
    k = h // g
    l = SSM_CHUNK
    c = s // l
    x = (xh.astype(jnp.float32) * dt[..., None]).reshape(b, c, l, g, k, p)
    la = (dt * a).reshape(b, c, l, g, k).transpose(0, 1, 3, 4, 2)
    acs = jnp.cumsum(la, axis=-1)
    bc = bm.astype(jnp.float32).reshape(b, c, l, g, n)
    cc = cm.astype(jnp.float32).reshape(b, c, l, g, n)
    seg = acs[..., :, None] - acs[..., None, :]
    causal = jnp.tril(jnp.ones((l, l), dtype=bool))
    decay = jnp.exp(jnp.where(causal, seg, -jnp.inf))
    cb = jnp.einsum('bclgn,bcsgn->bcgls', cc, bc)
    y_diag = jnp.einsum('bcgls,bcgkls,bcsgkp->bclgkp', cb, decay, x)
    decay_to_end = jnp.exp(acs[..., -1:] - acs)
    states = jnp.einsum('bclgn,bcgkl,bclgkp->bcgkpn', bc, decay_to_end, x)
    chunk_decay = jnp.exp(acs[..., -1])

    def step(carry, inp):
        st, dec = inp
        return carry * dec[..., None, None] + st, carry

    init = jnp.zeros((b, g, k, p, n), jnp.float32)
    _, prev = lax.scan(step, init, (jnp.moveaxis(states, 1, 0), jnp.moveaxis(chunk_decay, 1, 0)))
    prev = jnp.moveaxis(prev, 0, 1)
    y_off = jnp.einsum('bclgn,bcgkpn,bcgkl->bclgkp', cc, prev, jnp.exp(acs))
    return (y_diag + y_off).reshape(b, s, h, p)


def mamba2_branch(z, xbc, dt_raw, conv_w, conv_b, dt_bias, a_log, d_skip, norm_g, w_out):
    xbc = jax.nn.silu(causal_dwconv(xbc, conv_w) + conv_b.astype(xbc.dtype))
    xs, bm, cm = jnp.split(xbc, (SSM_D_INNER, SSM_D_INNER + SSM_GROUPS * SSM_STATE), axis=-1)
    b, s, _ = xs.shape
    xh = xs.reshape(b, s, SSM_HEADS, SSM_HEAD_DIM)
    dt = jax.nn.softplus(dt_raw.astype(jnp.float32) + dt_bias.astype(jnp.float32))
    a = -jnp.exp(a_log.astype(jnp.float32))
    y = ssd_chunked(xh, dt, a,
                    bm.reshape(b, s, SSM_GROUPS, SSM_STATE),
                    cm.reshape(b, s, SSM_GROUPS, SSM_STATE))
    y = y + d_skip.astype(jnp.float32)[:, None] * xh.astype(jnp.float32)
    yg = (y.reshape(b, s, SSM_D_INNER) * jax.nn.silu(z.astype(jnp.float32)))
    yg = yg.reshape(b, s, SSM_GROUPS, SSM_D_INNER // SSM_GROUPS)
    yg = yg * lax.rsqrt(jnp.mean(yg * yg, axis=-1, keepdims=True) + EPS)
    y = (yg.reshape(b, s, SSM_D_INNER) * norm_g.astype(jnp.float32)).astype(z.dtype)
    return y @ w_out


def memory_cross_attention(u, mem_n, w_q, w_kv, w_o):
    b, s, _ = u.shape
    m = mem_n.shape[1]
    q = (u @ w_q).reshape(b, s, XATTN_HEADS, XATTN_HEAD_DIM)
    k, v = jnp.split(mem_n @ w_kv, 2, axis=-1)
    k = k.reshape(b, m, XATTN_HEADS, XATTN_HEAD_DIM)
    v = v.reshape(b, m, XATTN_HEADS, XATTN_HEAD_DIM)
    scores = jnp.einsum('bshd,bmhd->bhsm', q, k).astype(jnp.float32) * XATTN_SCALE
    probs = jax.nn.softmax(scores, axis=-1).astype(v.dtype)
    o = jnp.einsum('bhsm,bmhd->bshd', probs, v).reshape(b, s, D_MODEL)
    return o @ w_o


def setup_inputs(seed: int = 0) -> dict:
    key = jax.random.key(seed)
    ks = iter(jax.random.split(key, 40))

    def w(shape, fan_in):
        return jax.random.normal(next(ks), shape, jnp.float32) * (fan_in ** -0.5)

    def gain(shape):
        return 1.0 + 0.02 * jax.random.normal(next(ks), shape, jnp.float32)

    L = DEPTH
    x = jax.random.normal(next(ks), (BATCH, SEQ, D_MODEL), jnp.float32)
    mem = jax.random.normal(next(ks), (BATCH, MEM_LEN, D_MODEL), jnp.float32)
    dt0 = jnp.exp(jax.random.uniform(next(ks), (L, SSM_HEADS), jnp.float32,
                                     math.log(1e-3), math.log(1e-1)))
    dt_bias = dt0 + jnp.log(-jnp.expm1(-dt0))
    a_log = jnp.log(jax.random.uniform(next(ks), (L, SSM_HEADS), jnp.float32, 1.0, 16.0))
    return {
        "x": x,
        "mem": mem,
        "ffn1_norm": gain((L, D_MODEL)),
        "ffn1_w_gate_up": w((L, D_MODEL, 2 * D_FF), D_MODEL),
        "ffn1_w_down": w((L, D_FF, D_MODEL), D_FF),
        "mix_norm": gain((L, D_MODEL)),
        "w_in": w((L, D_MODEL, D_IN_PROJ), D_MODEL),
        "conv_a_w": w((L, CONV_A_K, CONV_A_WIDTH), CONV_A_K),
        "w_out_a": w((L, CONV_A_WIDTH, D_MODEL), CONV_A_WIDTH),
        "ssm_conv_w": w((L, SSM_CONV_K, SSM_CONV_CH), SSM_CONV_K),
        "ssm_conv_b": 0.02 * jax.random.normal(next(ks), (L, SSM_CONV_CH), jnp.float32),
        "ssm_dt_bias": dt_bias,
        "ssm_a_log": a_log,
        "ssm_d": gain((L, SSM_HEADS)),
        "ssm_norm": gain((L, SSM_D_INNER)),
        "w_out_ssm": w((L, SSM_D_INNER, D_MODEL), SSM_D_INNER),
        "w_mix_out": w((L, D_MODEL, D_MODEL), D_MODEL),
        "xattn_norm": gain((L, D_MODEL)),
        "mem_norm": gain((L, D_MODEL)),
        "w_q": w((L, D_MODEL, D_MODEL), D_MODEL),
        "w_kv": w((L, D_MODEL, 2 * D_MODEL), D_MODEL),
        "w_o_x": w((L, D_MODEL, D_MODEL), D_MODEL),
        "ffn2_norm": gain((L, D_MODEL)),
        "ffn2_w_gate_up": w((L, D_MODEL, 2 * D_FF), D_MODEL),
        "ffn2_w_down": w((L, D_FF, D_MODEL), D_FF),
        "final_norm": gain((D_MODEL,)),
    }


def reference(x, mem, ffn1_norm, ffn1_w_gate_up, ffn1_w_down, mix_norm, w_in, conv_a_w,
              w_out_a, ssm_conv_w, ssm_conv_b, ssm_dt_bias, ssm_a_log, ssm_d, ssm_norm,
              w_out_ssm, w_mix_out, xattn_norm, mem_norm, w_q, w_kv, w_o_x,
              ffn2_norm, ffn2_w_gate_up, ffn2_w_down, final_norm):
    h = x
    for i in range(DEPTH):
        h = h + FFN_RES_WEIGHT * swiglu(rmsnorm(h, ffn1_norm[i]), ffn1_w_gate_up[i], ffn1_w_down[i])
        u = rmsnorm(h, mix_norm[i])
        proj = u @ w_in[i]
        a_b, a_c, a_v, z, xbc, dt_raw, g_a, g_b = jnp.split(proj, IN_SPLITS, axis=-1)
        y_a = short_conv_branch(a_b, a_c, a_v, conv_a_w[i], w_out_a[i])
        y_b = mamba2_branch(z, xbc, dt_raw, ssm_conv_w[i], ssm_conv_b[i], ssm_dt_bias[i],
                            ssm_a_log[i], ssm_d[i], ssm_norm[i], w_out_ssm[i])
        merged = jax.nn.sigmoid(g_a) * y_a + jax.nn.sigmoid(g_b) * y_b
        h = h + merged @ w_mix_out[i]
        h = h + memory_cross_attention(rmsnorm(h, xattn_norm[i]), rmsnorm(mem, mem_norm[i]),
                                       w_q[i], w_kv[i], w_o_x[i])
        h = h + FFN_RES_WEIGHT * swiglu(rmsnorm(h, ffn2_norm[i]), ffn2_w_gate_up[i], ffn2_w_down[i])
    return rmsnorm(h, final_norm)
```

```python
import numpy as np
from contextlib import ExitStack
import concourse.bass as bass
import concourse.mybir as mybir
from concourse.bass_utils import run_bass_kernel_spmd

F32 = mybir.dt.float32
BF16 = mybir.dt.bfloat16
ALU = mybir.AluOpType
AF = mybir.ActivationFunctionType

D = 1024
SEQ = 2048
MEM = 256
DFF = 2816
NJ = DFF // 128
DIN = 10272
EPS = 1e-6
C_AB, C_AC, C_AV, C_Z, C_XBC, C_DT, C_GA, C_GB = 0, 1024, 2048, 3072, 5120, 8192, 8224, 9248

CP_G = {"ffn1": 0, "mix": 8, "xattn": 16, "mem": 24, "ffn2": 32, "final": 40}
CP_CAW = 48
CP_SCW = 72
CP_SCB = 168
CP_GN = 192
CP_D = 208
CP_DTB = 224
CP_ALOG = 256
CP_GFB = 288
NCP = 288 + 1024


class Op:
    __slots__ = ("eng", "fn", "reads", "writes", "dma", "deps", "token", "waits", "idx", "n_inc", "has_dep")

    def __init__(self, eng, fn, reads, writes, dma):
        self.eng = eng
        self.fn = fn
        self.reads = tuple(reads)
        self.writes = tuple(writes)
        self.dma = dma
        self.deps = ()
        self.token = None
        self.waits = ()
        self.has_dep = False
        self.n_inc = 1


class Sched:
    ENGS = ("pe", "act", "dve", "pool", "sp")

    def __init__(self, nc, stack):
        self.nc = nc
        self.stack = stack
        self.ops = []
        self.sems = {}

    def sem(self, key):
        if key not in self.sems:
            self.sems[key] = self.stack.enter_context(self.nc.semaphore("s%d" % len(self.sems)))
        return self.sems[key]

    def op(self, eng, fn, reads=(), writes=(), dma=None):
        o = Op(eng, fn, reads, writes, dma)
        self.ops.append(o)
        return o

    def dma(self, eng, fn, reads=(), writes=(), key="d", n=1):
        o = self.op(eng, fn, reads, writes, dma=key)
        o.n_inc = n
        return o

    def finalize(self):
        ops = self.ops
        last_writer = {}
        readers = {}
        for i, o in enumerate(ops):
            o.idx = i
            deps = set()
            for k in o.reads:
                w = last_writer.get(k)
                if w is not None:
                    deps.add(w)
            for k in o.writes:
                w = last_writer.get(k)
                if w is not None:
                    deps.add(w)
                rl = readers.get(k)
                if rl:
                    deps.update(rl)
            deps.discard(i)
            for k in o.reads:
                readers.setdefault(k, []).append(i)
            for k in o.writes:
                last_writer[k] = i
                readers[k] = []
            fd = []
            for d in deps:
                p = ops[d]
                if p.dma is None and o.dma is None and p.eng == "pe" and o.eng == "pe":
                    continue
                fd.append(d)
                p.has_dep = True
            o.deps = fd
        cnt = {e: 0 for e in self.ENGS}
        dcnt = {}
        for o in ops:
            if o.dma is not None:
                dcnt[o.dma] = dcnt.get(o.dma, 0) + 16 * o.n_inc
                o.token = (("dma", o.dma), dcnt[o.dma])
            elif o.has_dep:
                cnt[o.eng] += 1
                o.token = (("eng", o.eng), cnt[o.eng])
        waited = {e: {} for e in self.ENGS}
        for o in ops:
            need = {}
            for d in o.deps:
                sk, v = ops[d].token
                if need.get(sk, 0) < v:
                    need[sk] = v
            w = waited[o.eng]
            ws = []
            for sk, v in need.items():
                if w.get(sk, 0) < v:
                    w[sk] = v
                    ws.append((sk, v))
            o.waits = ws

    def emit(self):
        nc = self.nc
        self.finalize()
        for e in self.ENGS:
            self.sem(("eng", e))
        for o in self.ops:
            if o.dma is not None:
                self.sem(("dma", o.dma))
        by_eng = {e: [o for o in self.ops if o.eng == e] for e in self.ENGS}
        sems = self.sems

        def run(eng_name, eng):
            mysem = sems[("eng", eng_name)]
            for o in by_eng[eng_name]:
                for sk, v in o.waits:
                    eng.wait_ge(sems[sk], v)
                ins = o.fn(eng)
                if o.dma is not None:
                    if not isinstance(ins, (list, tuple)):
                        ins = [ins]
                    assert len(ins) == o.n_inc, (len(ins), o.n_inc)
                    s = sems[("dma", o.dma)]
                    for x in ins:
                        x.then_inc(s, 16)
                elif o.token is not None:
                    ins.then_inc(mysem, 1)

        with nc.Block() as block:
            @block.tensor
            def _(e):
                run("pe", e)

            @block.scalar
            def _(e):
                run("act", e)

            @block.vector
            def _(e):
                run("dve", e)

            @block.gpsimd
            def _(e):
                run("pool", e)

            @block.sync
            def _(e):
                run("sp", e)


BIGW = [("ffn1_w_gate_up", D, 2 * DFF), ("ffn1_w_down", DFF, D), ("w_in", D, DIN), ("w_out_a", D, D),
        ("w_out_ssm", 2 * D, D), ("w_mix_out", D, D), ("w_q", D, D), ("w_kv", D, 2 * D), ("w_o_x", D, D),
        ("ffn2_w_gate_up", D, 2 * DFF), ("ffn2_w_down", DFF, D)]


CAST_ORDER = ["w_kv", "ffn1_w_gate_up", "ffn1_w_down", "w_in", "w_out_ssm", "w_out_a", "w_mix_out", "w_q", "w_o_x", "ffn2_w_gate_up", "ffn2_w_down"]


class Builder:
    def __init__(self, nc, st, NSEQ, NT, T, dbg=()):
        self.nc, self.st, self.NSEQ, self.NT, self.T = nc, st, NSEQ, NT, T
        self.QT = T // 128
        self.S = Sched(nc, st)
        self.dbg = set(dbg)
        self.dbg_out = {}
        self.psi = 0
        self.wsi = 0
        self.ev = 0
        self.slab_use = []
        self.cast_order = None
        self.dry = False

    def sb(self, name, shape, dt):
        return self.st.enter_context(self.nc.sbuf_tensor(name, shape, dt))

    def cvk(self, m):
        return [("cv", m), ("raw", 0), ("raw", 1)] + [("ta", j) for j in range(4)]

    def iok(self, q):
        return [("W3", q * self.cpq + i) for i in range(self.cpq)]

    def nb(self):
        b = self.psi % 8
        self.psi += 1
        return b

    def dump(self, name, ap, keys, shape):
        if name not in self.dbg:
            return
        d = self.nc.dram_tensor("dbg_" + name, list(shape), F32, kind="ExternalOutput").ap()
        stg = self.sb("dbgs_" + name, list(shape), F32)
        self.S.op("dve", lambda e: e.tensor_copy(out=stg[:], in_=ap), reads=keys, writes=["dbgs_" + name])
        self.S.dma("pool", lambda e: e.dma_start(out=d, in_=stg[:]), reads=["dbgs_" + name], writes=["dbgd_" + name], key="dbg")
        self.dbg_out[name] = "dbgd_" + name

    def wslab(self, wname, r0, nkc, c0, ncols):
        i = self.wsi % self.NWS
        self.wsi += 1
        t = self.ws[i]
        key = ("ws", i)
        for blk in range(c0 // 512, (c0 + ncols - 1) // 512 + 1):
            if (wname, blk) not in self.slab_use:
                self.slab_use.append((wname, blk))
        src = self.wbf[wname][r0 * 128:(r0 + nkc) * 128, c0:c0 + ncols].rearrange("(k p) n -> p k n", p=128)
        dst = t[:, 0:nkc, 0:ncols]
        self.S.dma("sp", lambda e: e.dma_start(out=dst, in_=src), reads=[("wbf", wname, blk) for blk in range(c0 // 512, (c0 + ncols - 1) // 512 + 1)], writes=[key], key=("w", i))
        return t, key

    def build(self):
        nc, S, T, QT = self.nc, self.S, self.T, self.QT
        NSEQ, NT = self.NSEQ, self.NT
        sb = self.sb
        dram = lambda n, s, dt, kind: nc.dram_tensor(n, s, dt, kind=kind).ap()
        self.x_d = dram("x", [NSEQ, SEQ, D], F32, "ExternalInput")
        self.mem_d = dram("mem", [NSEQ, MEM, D], F32, "ExternalInput")
        self.cpk_d = dram("cpk", [128, NCP], F32, "ExternalInput")
        self.y_d = dram("y", [NSEQ, NT * T, D], F32, "ExternalOutput")
        self.wf32 = {n: dram(n, [k, m], F32, "ExternalInput") for n, k, m in BIGW}
        self.wbf = {n: dram(n + "_bf", [k, m], BF16, "Internal") for n, k, m in BIGW}

        self.PS = self.st.enter_context(nc.psum_tensor("PS", [128, 8, 512], F32))
        PS = self.PS
        self.NWS = 4
        self.ws = [sb("ws%d" % i, [128, 8, 512], BF16) for i in range(self.NWS)]
        self.cp = sb("cp", [128, NCP], F32)
        self.idf = sb("idf", [128, 128], F32)
        self.idb = sb("idb", [128, 128], BF16)
        self.onesf = sb("onesf", [128, 128], F32)
        self.onesb = sb("onesb", [128, 128], BF16)
        self.Uf = sb("Uf", [128, 128], F32)
        self.Lf = sb("Lf", [128, 128], F32)
        self.Lb = sb("Lb", [128, 128], BF16)
        self.Ub = sb("Ub", [128, 128], BF16)
        self.diagD = sb("diagD", [128, 16, 128], BF16)
        self.aneg = sb("aneg", [128, 32], F32)
        self.io = sb("io", [128, QT, D], F32)
        self.h = sb("h", [128, 8, T], F32)
        self.u = sb("u", [128, 8, T], BF16)
        self.W2 = sb("W2", [128, 24, T + 4], BF16)
        self.W1 = sb("W1", [128, NJ, T], BF16)
        self.W3 = self.io[:].bitcast(BF16).rearrange("p q (c n) -> p (q c) n", n=T)
        self.cpq = 16 // QT
        self.xin = self.W1[:, 0:16, :].bitcast(F32).rearrange("p (q c) n -> p q (c n)", q=QT)
        self.loaded = None
        self.rs = [sb("rs%d" % i, [128, T], F32) for i in range(2)]
        self.tf = [sb("tf%d" % i, [128, T], F32) for i in range(2)]
        self.arena = sb("arena", [128, 8 * (T + 2)], BF16)
        rw = 2 * (T + 4)
        self.raw = [self.arena[:, i * rw:(i + 1) * rw].bitcast(F32) for i in range(2)]
        self.cv = self.arena[:, :].rearrange("p (c n) -> p c n", n=T + 2)
        self.histx = sb("histx", [128, 24, 3], F32)
        self.hista = sb("hista", [128, 8, 2], BF16)
        self.tab = [self.arena[:, 4 * (T + 4):4 * (T + 4) + 4 * T].rearrange("p (c n) -> p c n", n=T)] * 2
        self.mg = sb("mg", [128, 4, T], F32)
        self.dtt = sb("dtt", [128, QT, 32], F32)
        self.kmax2 = sb("kmax2", [128, 4], F32)
        self.negc = sb("negc", [128, 4], F32)
        self.ssq = sb("ssq", [128, 2 * QT], F32)
        self.ssum = sb("ssum", [128, QT], F32)
        self.lat = sb("lat", [128, QT, 32], F32)
        self.latb = sb("latb", [128, QT, 32], BF16)
        self.sp1 = sb("sp1", [128, QT, 32], F32)
        self.sp2 = sb("sp2", [128, QT, 32], F32)
        self.exs = [sb("ex%d" % i, [128, 3, 32], F32) for i in range(2)]
        self.dds = [sb("dd%d" % i, [128, 32], F32) for i in range(2)]
        self.R = [sb("R%d" % i, [128, 8, 128], BF16) for i in range(2)]
        self.E = [sb("E%d" % i, [128, 8, 128], BF16) for i in range(2)]
        self.MT = [sb("MT%d" % i, [128, 8, 128], BF16) for i in range(2)]
        self.xdt = sb("xdt", [128, 32, 64], BF16)
        self.xdte = sb("xdte", [128, 32, 64], BF16)
        self.Btm = sb("Btm", [128, 4, 128], BF16)
        self.CBm = sb("CBm", [128, 4, 128], BF16)
        self.yo = [sb("yo%d" % i, [128, 8, 64], BF16) for i in range(2)]
        self.state = sb("state", [128, 32, 64], F32)
        self.prevb = sb("prevb", [128, 32, 64], BF16)
        self.KT = sb("KT", [128, 8, MEM], BF16)
        self.V = sb("V", [128, 2, D], BF16)
        self.ET = [sb("ET%d" % i, [128, 2, T], BF16) for i in range(2)]

        self.setup()
        for s in range(NSEQ):
            self.seq_setup(s)
            for t in range(NT):
                self.tile(s, t)
        S.op("pool", lambda e: e.nop(), reads=["y_out"] + list(self.dbg_out.values()))
        if not self.dry:
            S.emit()

    def setup(self):
        S, nc = self.S, self.nc
        cp = self.cp
        S.dma("pool", lambda e: e.dma_start(out=cp[:], in_=self.cpk_d), writes=["cp"], key="cp")
        idf, idb, onesf, onesb, Uf, Lf = self.idf, self.idb, self.onesf, self.onesb, self.Uf, self.Lf
        S.op("pool", lambda e: e.memset(idf[:], 0.0), writes=["idf"])
        S.op("pool", lambda e: e.affine_select(out=idf[:], in_=idf[:], compare_op=ALU.not_equal, fill=1.0, base=0,
                                                pattern=[[-1, 128]], channel_multiplier=1), reads=["idf"], writes=["idf"])
        S.op("pool", lambda e: e.memset(onesf[:], 1.0), writes=["onesf"])
        S.op("pool", lambda e: e.memset(onesb[:], 1.0), writes=["onesb"])
        S.op("pool", lambda e: e.affine_select(out=Uf[:], in_=onesf[:], compare_op=ALU.is_ge, fill=0.0, base=0,
                                                pattern=[[1, 128]], channel_multiplier=-1), reads=["onesf"], writes=["Uf"])
        S.op("pool", lambda e: e.affine_select(out=Lf[:], in_=onesf[:], compare_op=ALU.is_gt, fill=0.0, base=0,
                                                pattern=[[-1, 128]], channel_multiplier=1), reads=["onesf"], writes=["Lf"])
        S.op("dve", lambda e: e.tensor_copy(out=idb[:], in_=idf[:]), reads=["idf"], writes=["idb"])
        S.op("dve", lambda e: e.tensor_copy(out=self.Lb[:], in_=Lf[:]), reads=["Lf"], writes=["Lb"])
        S.op("dve", lambda e: e.tensor_copy(out=self.Ub[:], in_=Uf[:]), reads=["Uf"], writes=["Ub"])
        for c in range(16):
            S.op("dve", lambda e, c=c: e.tensor_scalar(out=self.diagD[:, c, :], in0=idf[:], scalar1=cp[:, CP_D + c:CP_D + c + 1],
                                                        scalar2=None, op0=ALU.mult), reads=["idf", "cp"], writes=["diagD"])
        S.op("act", lambda e: e.activation(out=self.aneg[:], in_=cp[:, CP_ALOG:CP_ALOG + 32], func=AF.Exp), reads=["cp"], writes=["aneg"])
        S.op("dve", lambda e: e.tensor_scalar(out=self.aneg[:], in0=self.aneg[:], scalar1=-1.0, scalar2=None, op0=ALU.mult),
             reads=["aneg"], writes=["aneg"])
        self.issue_load(("mem", 0), self.mem_d[0], MEM)
        dims = {n: (k, m) for n, k, m in BIGW}
        order = list(self.cast_order or [])
        for n, k, m in BIGW:
            for blk in range((m + 511) // 512):
                if (n, blk) not in order:
                    order.append((n, blk))
        for n, blk in order:
            m = dims[n][1]
            src, dst = self.wf32[n], self.wbf[n]
            c0, c1 = blk * 512, min(m, blk * 512 + 512)
            S.dma("pool", lambda e, src=src, dst=dst, c0=c0, c1=c1: e.dma_start(out=dst[:, c0:c1], in_=src[:, c0:c1]),
                  writes=[("wbf", n, blk)], key=("cast", n, blk))

    def evac_eng(self):
        self.ev += 1
        return "act" if self.ev % 2 else "dve"

    def copy(self, eng, out, in_, reads, writes, scale=None):
        if eng == "act":
            if scale is None:
                self.S.op("act", lambda e: e.activation(out=out, in_=in_, func=AF.Copy), reads=reads, writes=writes)
            else:
                self.S.op("act", lambda e: e.activation(out=out, in_=in_, func=AF.Copy, scale=scale), reads=reads, writes=writes)
        else:
            if scale is None:
                self.S.op(eng, lambda e: e.tensor_copy(out=out, in_=in_), reads=reads, writes=writes)
            else:
                self.S.op(eng, lambda e: e.tensor_scalar(out=out, in0=in_, scalar1=scale, scalar2=None, op0=ALU.mult), reads=reads, writes=writes)

    def xk(self, q):
        return [("W1", q * self.cpq + i) for i in range(self.cpq)]

    def issue_load(self, tag, src_rows, nrows):
        nq = nrows // 128
        xin = self.xin
        self.S.dma("act", lambda e: e.dma_start(out=xin[:, 0:nq, :], in_=src_rows.rearrange("(q p) d -> p q d", p=128)),
                   writes=[k for q in range(nq) for k in self.xk(q)], key="io")
        self.loaded = tag

    def load_fm(self, tag, src_rows, nrows, dst, dst_key, ncols_off=0):
        S, PS, xin = self.S, self.PS, self.xin
        nq = nrows // 128
        if self.loaded != tag:
            self.issue_load(tag, src_rows, nrows)
        for c in range(8):
            b = self.nb()
            for q in range(nq):
                S.op("pe", lambda e, c=c, q=q, b=b: e.transpose(PS[:, b, q * 128:(q + 1) * 128], xin[:, q, c * 128:(c + 1) * 128], self.idf[:]),
                     reads=self.xk(q) + ["idf"], writes=[("ps", b)])
            self.copy(self.evac_eng(), dst[:, c, ncols_off:ncols_off + nrows], PS[:, b, 0:nrows], [("ps", b)], [(dst_key, c)])

    def rmsnorm(self, src, src_key, C, gcol, dst, dst_key, n, sq, sq_key, dim, src_is_bf=False):
        S, PS = self.S, self.PS
        for c in range(C):
            S.op("act", lambda e, c=c: e.activation(out=sq[:, c, 0:n], in_=src[:, c, 0:n], func=AF.Square),
                 reads=[(src_key, c)], writes=[(sq_key, c)])
        b = self.nb()
        for c in range(C):
            S.op("pe", lambda e, c=c: e.matmul(PS[:, b, 0:n], lhsT=self.onesb[:], rhs=sq[:, c, 0:n], start=(c == 0), stop=(c == C - 1)),
                 reads=[(sq_key, c), "onesb"], writes=[("ps", b)])
        r = self.rs[self.ev % 2]
        rk = ("rs", self.ev % 2)
        self.ev += 1
        S.op("act", lambda e: e.activation(out=r[:, 0:n], in_=PS[:, b, 0:n], func=AF.Ln, bias=self.epsb[:, 0:1], scale=1.0 / dim),
             reads=[("ps", b), "epsb"], writes=[rk])
        S.op("act", lambda e: e.activation(out=r[:, 0:n], in_=r[:, 0:n], func=AF.Exp, scale=-0.5), reads=[rk], writes=[rk])
        for c in range(C):
            S.op("dve", lambda e, c=c: e.scalar_tensor_tensor(out=dst[:, c, 0:n], in0=src[:, c, 0:n], scalar=self.cp[:, gcol + c:gcol + c + 1],
                                                               in1=r[:, 0:n], op0=ALU.mult, op1=ALU.mult),
                 reads=[(src_key, c), rk, "cp"], writes=[(dst_key, c)])

    def linear_gen(self, wname, c0, nout, rhs, rhs_key, evac, n, KC=8):
        S, PS = self.S, self.PS
        m = 0
        while m < nout:
            nm = min(4, nout - m)
            t, key = self.wslab(wname, 0, KC, c0 + m * 128, nm * 128)
            for mm in range(nm):
                b = self.nb()
                for k in range(KC):
                    S.op("pe", lambda e, t=t, mm=mm, k=k, b=b: e.matmul(PS[:, b, 0:n], lhsT=t[:, k, mm * 128:(mm + 1) * 128], rhs=rhs[:, k, 0:n],
                                                                         start=(k == 0), stop=(k == KC - 1)),
                         reads=[key, (rhs_key, k)], writes=[("ps", b)])
                evac(m + mm, b)
                yield
            m += nm

    def linear(self, *a, **kw):
        for _ in self.linear_gen(*a, **kw):
            pass

    def interleave(self, gens, ratio=None):
        gens = list(gens)
        ratio = ratio or [1] * len(gens)
        alive = [True] * len(gens)
        while any(alive):
            for i, g in enumerate(gens):
                for _ in range(ratio[i]):
                    if alive[i]:
                        try:
                            next(g)
                        except StopIteration:
                            alive[i] = False

    def linear_bigk(self, wname, KCT, rhs, rhs_key, evac, n):
        S, PS = self.S, self.PS
        for hf in range(2):
            banks = [self.nb() for _ in range(4)]
            k0 = 0
            while k0 < KCT:
                nk = min(8, KCT - k0)
                t, key = self.wslab(wname, k0, nk, hf * 512, 512)
                for mm in range(4):
                    for k in range(nk):
                        S.op("pe", lambda e, t=t, mm=mm, k=k, kk=k0 + k, b=banks[mm]: e.matmul(
                            PS[:, b, 0:n], lhsT=t[:, k, mm * 128:(mm + 1) * 128], rhs=rhs[:, kk, 0:n], start=(kk == 0), stop=(kk == KCT - 1)),
                            reads=[key, (rhs_key, k0 + k)], writes=[("ps", banks[mm])])
                k0 += nk
            for mm in range(4):
                evac(hf * 4 + mm, banks[mm])

    def resid_add(self, m, b, scale, n):
        h, PS = self.h, self.PS
        self.S.op("dve", lambda e: e.scalar_tensor_tensor(out=h[:, m, 0:n], in0=PS[:, b, 0:n], scalar=scale, in1=h[:, m, 0:n],
                                                          op0=ALU.mult, op1=ALU.add), reads=[("ps", b), ("h", m)], writes=[("h", m)])

    def ffn(self, which):
        S, PS, T = self.S, self.PS, self.T
        W1, u = self.W1, self.u
        self.rmsnorm(self.h, "h", 8, CP_G[which], u, "u", T, self.W3, "W3", D)
        wgu, wd = which + "_w_gate_up", which + "_w_down"
        j = 0
        while j < NJ:
            nj = min(4, NJ - j)
            tg, kg = self.wslab(wgu, 0, 8, j * 128, nj * 128)
            tu, ku = self.wslab(wgu, 0, 8, DFF + j * 128, nj * 128)
            for jj in range(nj):
                bg, bu = self.nb(), self.nb()
                for (t, key, b) in ((tg, kg, bg), (tu, ku, bu)):
                    for k in range(8):
                        S.op("pe", lambda e, t=t, jj=jj, k=k, b=b: e.matmul(PS[:, b, 0:T], lhsT=t[:, k, jj * 128:(jj + 1) * 128], rhs=u[:, k, :],
                                                                             start=(k == 0), stop=(k == 7)),
                             reads=[key, ("u", k)], writes=[("ps", b)])
                tfi = (j + jj) % 2
                tf = self.tf[tfi]
                S.op("act", lambda e, tf=tf, bg=bg: e.activation(out=tf[:], in_=PS[:, bg, 0:T], func=AF.Silu), reads=[("ps", bg)], writes=[("tf", tfi)])
                S.op("dve", lambda e, tf=tf, bu=bu, jx=j + jj: e.tensor_tensor(out=W1[:, jx, :], in0=tf[:], in1=PS[:, bu, 0:T], op=ALU.mult),
                     reads=[("tf", tfi), ("ps", bu)], writes=[("W1", j + jj)])
            j += nj
        self.linear_bigk(wd, NJ, W1, "W1", lambda m, b: self.resid_add(m, b, 0.5, T), T)

    def seq_setup(self, s):
        S, PS = self.S, self.PS
        S.op("pool", lambda e: e.memset(self.state[:], 0.0), writes=[("state", g) for g in range(4)])
        S.op("pool", lambda e: e.memset(self.prevb[:], 0.0), writes=[("prevb", g) for g in range(4)])
        S.op("pool", lambda e: e.memset(self.histx[:], 0.0), writes=[("histx", c) for c in range(24)])
        S.op("pool", lambda e: e.memset(self.hista[:], 0.0), writes=[("hista", c) for c in range(8)])
        self.load_fm(("mem", s), self.mem_d[s], MEM, self.h, "h")
        self.rmsnorm(self.h, "h", 8, CP_G["mem"], self.u, "u", MEM, self.W3, "W3", D)
        KT, V, u = self.KT, self.V, self.u

        def evK(m, b):
            self.copy(self.evac_eng(), KT[:, m, :], PS[:, b, 0:MEM], [("ps", b)], [("KT", m)])
        self.linear("w_kv", 0, 8, u, "u", evK, MEM)
        for hd in range(4):
            for dd in range(2):
                c = 2 * hd + dd
                S.op("act", lambda e, c=c: e.activation(out=self.W3[:, c, 0:MEM], in_=KT[:, c, :], func=AF.Square), reads=[("KT", c)], writes=[("W3", c)])
            b = self.nb()
            for dd in range(2):
                c = 2 * hd + dd
                S.op("pe", lambda e, c=c, dd=dd, b=b: e.matmul(PS[:, b, 0:MEM], lhsT=self.onesb[:], rhs=self.W3[:, c, 0:MEM], start=(dd == 0), stop=(dd == 1)),
                     reads=[("W3", c), "onesb"], writes=[("ps", b)])
            S.op("dve", lambda e, hd=hd, b=b: e.tensor_reduce(out=self.kmax2[:, hd:hd + 1], in_=PS[:, b, 0:MEM], axis=mybir.AxisListType.X, op=ALU.max),
                 reads=[("ps", b)], writes=[("kmax2", hd)])
        for hf in range(2):
            t, key = self.wslab("w_kv", 0, 8, D + hf * 512, 512)
            for mc in range(2):
                b = self.nb()
                for k in range(8):
                    S.op("pe", lambda e, t=t, k=k, mc=mc, b=b: e.matmul(PS[:, b, :], lhsT=u[:, k, mc * 128:(mc + 1) * 128], rhs=t[:, k, :],
                                                                         start=(k == 0), stop=(k == 7)),
                         reads=[key, ("u", k)], writes=[("ps", b)])
                self.copy(self.evac_eng(), V[:, mc, hf * 512:(hf + 1) * 512], PS[:, b, :], [("ps", b)], [("V", mc)])

    def tile(self, s, ti):
        S, PS, T, QT = self.S, self.PS, self.T, self.QT
        h, u, W1, W2, W3, cp = self.h, self.u, self.W1, self.W2, self.W3, self.cp
        t0 = ti * T
        self.load_fm(("x", s, ti), self.x_d[s, t0:t0 + T, :], T, h, "h")
        self.dump("h0", h[:], [("h", c) for c in range(8)], [128, 8, T])
        self.ffn("ffn1")
        self.dump("h1", h[:], [("h", c) for c in range(8)], [128, 8, T])
        self.rmsnorm(h, "h", 8, CP_G["mix"], u, "u", T, W3, "W3", D)
        tdt, kdt = self.wslab("w_in", 0, 8, C_DT, 32)
        bdt = self.nb()
        for q in range(QT):
            for k in range(8):
                S.op("pe", lambda e, q=q, k=k, bdt=bdt, tdt=tdt: e.matmul(PS[:, bdt, q * 32:(q + 1) * 32], lhsT=u[:, k, q * 128:(q + 1) * 128], rhs=tdt[:, k, 0:32],
                                                        start=(k == 0), stop=(k == 7)), reads=[kdt, ("u", k)], writes=[("ps", bdt)])
        dtt, lat, sp1, sp2 = self.dtt, self.lat, self.sp1, self.sp2
        psv = PS[:, bdt, 0:QT * 32].rearrange("p (q h) -> p q h", h=32)
        bias_b = cp[:, CP_DTB:CP_DTB + 32].unsqueeze(1).to_broadcast([128, QT, 32])
        S.op("dve", lambda e: e.tensor_tensor(out=sp1[:], in0=psv, in1=bias_b, op=ALU.add), reads=[("ps", bdt), "cp"], writes=["sp1"])
        S.op("act", lambda e: e.activation(out=sp2[:], in_=sp1[:], func=AF.Abs), reads=["sp1"], writes=["sp2"])
        S.op("act", lambda e: e.activation(out=sp2[:], in_=sp2[:], func=AF.Exp, scale=-1.0), reads=["sp2"], writes=["sp2"])
        S.op("act", lambda e: e.activation(out=sp2[:], in_=sp2[:], func=AF.Ln, bias=self.oneb[:, 0:1]), reads=["sp2", "oneb"], writes=["sp2"])
        S.op("dve", lambda e: e.scalar_tensor_tensor(out=dtt[:], in0=sp1[:], scalar=0.0, in1=sp2[:], op0=ALU.max, op1=ALU.add),
             reads=["sp1", "sp2"], writes=["dtt"])
        S.op("dve", lambda e: e.tensor_tensor(out=lat[:], in0=dtt[:], in1=self.aneg[:].unsqueeze(1).to_broadcast([128, QT, 32]), op=ALU.mult),
             reads=["dtt", "aneg"], writes=["lat"])
        S.op("dve", lambda e: e.tensor_copy(out=self.latb[:], in_=lat[:]), reads=["lat"], writes=["latb"])
        self.dump("dtt", dtt[:], ["dtt"], [128, QT, 32])
        def evx(c, b):
            ri = c % 2
            raw, rk = self.raw[ri], ("raw", ri)
            S.op("pool", lambda e: e.tensor_copy(out=raw[:, 0:3], in_=self.histx[:, c, :]), reads=[("histx", c)], writes=[rk])
            S.op("act", lambda e: e.activation(out=raw[:, 3:3 + T], in_=PS[:, b, 0:T], func=AF.Copy), reads=[("ps", b)], writes=[rk])
            S.op("pool", lambda e: e.tensor_copy(out=self.histx[:, c, :], in_=raw[:, T:T + 3]), reads=[rk], writes=[("histx", c)])
            tf, tk = self.tf[ri], ("tf", ri)
            w = lambda k: cp[:, CP_SCW + k * 24 + c:CP_SCW + k * 24 + c + 1]
            S.op("act", lambda e: e.activation(out=tf[:], in_=PS[:, b, 0:T], func=AF.Identity, scale=w(3), bias=cp[:, CP_SCB + c:CP_SCB + c + 1]),
                 reads=[("ps", b), "cp"], writes=[tk])
            if self.pending_silu is not None:
                self.pending_silu()
            for k in range(3):
                S.op("dve", lambda e, k=k: e.scalar_tensor_tensor(out=tf[:], in0=raw[:, k:k + T], scalar=w(k), in1=tf[:], op0=ALU.mult, op1=ALU.add),
                     reads=[rk, tk, "cp"], writes=[tk])
            self.pending_silu = lambda: S.op("act", lambda e: e.activation(out=W2[:, c, 0:T], in_=tf[:], func=AF.Silu), reads=[tk], writes=[("W2", c)])
        def evz(m, b):
            S.op("act", lambda e: e.activation(out=W1[:, m, :], in_=PS[:, b, 0:T], func=AF.Silu), reads=[("ps", b)], writes=[("W1", m)])
        self.pending_silu = None
        self.interleave([self.linear_gen("w_in", C_XBC, 24, u, "u", evx, T), self.linear_gen("w_in", C_Z, 16, u, "u", evz, T)], ratio=[3, 1])
        self.pending_silu()
        self.dump("xbc", W2[:, :, 0:T], [("W2", c) for c in range(24)], [128, 24, T])
        def ev_ac(m, b):
            S.op("act", lambda e: e.activation(out=self.mg[:, m % 4, :], in_=PS[:, b, 0:T], func=AF.Copy), reads=[("ps", b)], writes=[("mg", m % 4)])
        def ev_av(m, b):
            S.op("pool", lambda e: e.tensor_copy(out=self.cv[:, m, 0:2], in_=self.hista[:, m, :]), reads=[("hista", m)], writes=self.cvk(m))
            S.op("dve", lambda e: e.tensor_tensor(out=self.cv[:, m, 2:2 + T], in0=self.mg[:, m % 4, :], in1=PS[:, b, 0:T], op=ALU.mult),
                 reads=[("mg", m % 4), ("ps", b)], writes=self.cvk(m))
        def agen():
            for hf in range(2):
                yield from self.linear_gen("w_in", C_AC + hf * 512, 4, u, "u", lambda m, b, hf=hf: ev_ac(m + hf * 4, b), T)
                yield from self.linear_gen("w_in", C_AV + hf * 512, 4, u, "u", lambda m, b, hf=hf: ev_av(m + hf * 4, b), T)
        self.agen = agen()
        self.ssd_tile()
        self.dump("yg", W3[:], [("W3", c) for c in range(16)], [128, 16, T])
        cv = self.cv
        def ev_ab(m, b):
            tfi = m % 2
            tf, tk = self.tf[tfi], ("tf", tfi)
            w = lambda k: cp[:, CP_CAW + k * 8 + m:CP_CAW + k * 8 + m + 1]
            S.op("dve", lambda e: e.tensor_scalar(out=tf[:], in0=cv[:, m, 0:T], scalar1=w(0), scalar2=None, op0=ALU.mult), reads=self.cvk(m) + ["cp"], writes=[tk])
            for k in (1, 2):
                S.op("dve", lambda e, k=k: e.scalar_tensor_tensor(out=tf[:], in0=cv[:, m, k:k + T], scalar=w(k), in1=tf[:], op0=ALU.mult, op1=ALU.add),
                     reads=self.cvk(m) + [tk, "cp"], writes=[tk])
            S.op("dve", lambda e: e.tensor_tensor(out=W2[:, m, 0:T], in0=tf[:], in1=PS[:, b, 0:T], op=ALU.mult),
                 reads=[tk, ("ps", b)], writes=[("W2", m)])
            S.op("pool", lambda e: e.tensor_copy(out=self.hista[:, m, :], in_=cv[:, m, T:T + 2]), reads=self.cvk(m), writes=[("hista", m)])
        self.group_norm(1)
        gab = self.linear_gen("w_in", C_AB, 8, u, "u", ev_ab, T)
        for _ in range(4):
            next(gab)
        self.group_norm(2)
        for _ in gab:
            pass
        self.group_norm(3)
        for hf in range(2):
            ta, tb = self.tab
            def ev_ga(m, b):
                S.op("act", lambda e: e.activation(out=ta[:, m, :], in_=PS[:, b, 0:T], func=AF.Tanh, scale=0.5), reads=[("ps", b)], writes=[("ta", m)])
            self.linear("w_in", C_GA + hf * 512, 4, u, "u", ev_ga, T)
            def ev_ya(m, b):
                S.op("dve", lambda e: e.scalar_tensor_tensor(out=self.mg[:, m, :], in0=ta[:, m, :], scalar=1.0, in1=PS[:, b, 0:T], op0=ALU.add, op1=ALU.mult),
                     reads=[("ta", m), ("ps", b)], writes=[("mg", m)])
            t, key = self.wslab("w_out_a", 0, 8, hf * 512, 512)
            for mm in range(4):
                b = self.nb()
                for k in range(8):
                    S.op("pe", lambda e, t=t, mm=mm, k=k, b=b: e.matmul(PS[:, b, 0:T], lhsT=t[:, k, mm * 128:(mm + 1) * 128], rhs=W2[:, k, 0:T],
                                                                         start=(k == 0), stop=(k == 7)), reads=[key, ("W2", k)], writes=[("ps", b)])
                ev_ya(mm, b)
            def ev_gb(m, b):
                S.op("act", lambda e: e.activation(out=tb[:, m, :], in_=PS[:, b, 0:T], func=AF.Tanh, scale=0.5), reads=[("ps", b)], writes=[("ta", m)])
            self.linear("w_in", C_GB + hf * 512, 4, u, "u", ev_gb, T)
            banks = [self.nb() for _ in range(4)]
            for ks in range(2):
                t, key = self.wslab("w_out_ssm", ks * 8, 8, hf * 512, 512)
                for mm in range(4):
                    for k in range(8):
                        kk = ks * 8 + k
                        S.op("pe", lambda e, t=t, mm=mm, k=k, kk=kk, b=banks[mm]: e.matmul(PS[:, b, 0:T], lhsT=t[:, k, mm * 128:(mm + 1) * 128], rhs=W3[:, kk, :],
                                                                                              start=(kk == 0), stop=(kk == 15)),
                             reads=[key, ("W3", kk)], writes=[("ps", banks[mm])])
            for mm in range(4):
                b = banks[mm]
                tfi = mm % 2
                tf, tk = self.tf[tfi], ("tf", tfi)
                S.op("dve", lambda e, mm=mm, b=b, tf=tf: e.scalar_tensor_tensor(out=tf[:], in0=tb[:, mm, :], scalar=1.0, in1=PS[:, b, 0:T], op0=ALU.add, op1=ALU.mult),
                     reads=[("ta", mm), ("ps", b)], writes=[tk])
                S.op("dve", lambda e, mm=mm, tf=tf, hf=hf: e.tensor_tensor(out=W1[:, hf * 4 + mm, :], in0=tf[:], in1=self.mg[:, mm, :], op=ALU.add),
                     reads=[tk, ("mg", mm)], writes=[("W1", hf * 4 + mm)])
        self.linear("w_mix_out", 0, 8, W1, "W1", lambda m, b: self.resid_add(m, b, 0.5, T), T)
        self.dump("h2", h[:], [("h", c) for c in range(8)], [128, 8, T])
        self.rmsnorm(h, "h", 8, CP_G["xattn"], u, "u", T, W3, "W3", D)
        qb = W1
        def ev_q(m, b):
            self.copy(self.evac_eng(), qb[:, m, :], PS[:, b, 0:T], [("ps", b)], [("W1", m)], scale=1.0 / 16.0)
        self.linear("w_q", 0, 8, u, "u", ev_q, T)
        KT, V = self.KT, self.V
        negc = self.negc
        for hd in range(4):
            for dd in range(2):
                c = 2 * hd + dd
                S.op("act", lambda e, c=c: e.activation(out=W3[:, c, :], in_=qb[:, c, :], func=AF.Square), reads=[("W1", c)], writes=[("W3", c)])
            bq = self.nb()
            for dd in range(2):
                c = 2 * hd + dd
                S.op("pe", lambda e, c=c, dd=dd, bq=bq: e.matmul(PS[:, bq, 0:T], lhsT=self.onesb[:], rhs=W3[:, c, :], start=(dd == 0), stop=(dd == 1)),
                     reads=[("W3", c), "onesb"], writes=[("ps", bq)])
            S.op("dve", lambda e, hd=hd, bq=bq: e.tensor_reduce(out=negc[:, hd:hd + 1], in_=PS[:, bq, 0:T], axis=mybir.AxisListType.X, op=ALU.max),
                 reads=[("ps", bq)], writes=[("negc", hd)])
        S.op("dve", lambda e: e.tensor_tensor(out=negc[:], in0=negc[:], in1=self.kmax2[:], op=ALU.mult),
             reads=[("negc", i) for i in range(4)] + [("kmax2", i) for i in range(4)], writes=[("negc", i) for i in range(4)])
        S.op("act", lambda e: e.activation(out=negc[:], in_=negc[:], func=AF.Ln, bias=self.epsb[:, 0:1]), reads=[("negc", i) for i in range(4)] + ["epsb"], writes=[("negc", i) for i in range(4)])
        S.op("act", lambda e: e.activation(out=negc[:], in_=negc[:], func=AF.Exp, scale=0.5), reads=[("negc", i) for i in range(4)], writes=[("negc", i) for i in range(4)])
        S.op("dve", lambda e: e.tensor_scalar(out=negc[:], in0=negc[:], scalar1=60.0, scalar2=-1.0, op0=ALU.min, op1=ALU.mult),
             reads=[("negc", i) for i in range(4)], writes=[("negc", i) for i in range(4)])
        def att_s1(hd):
            ET, ek = self.ET[hd % 2], "ET%d" % (hd % 2)
            for mc in range(2):
                b = self.nb()
                for dd in range(2):
                    S.op("pe", lambda e, mc=mc, dd=dd, b=b: e.matmul(PS[:, b, 0:T], lhsT=KT[:, 2 * hd + dd, mc * 128:(mc + 1) * 128], rhs=qb[:, 2 * hd + dd, :],
                                                                      start=(dd == 0), stop=(dd == 1)),
                         reads=[("KT", 2 * hd + dd), ("W1", 2 * hd + dd)], writes=[("ps", b)])
                S.op("act", lambda e, mc=mc, b=b: e.activation(out=ET[:, mc, :], in_=PS[:, b, 0:T], func=AF.Exp, bias=negc[:, hd:hd + 1]),
                     reads=[("ps", b), ("negc", hd)], writes=[(ek, mc)])

        def att_s2(hd):
            ET, ek = self.ET[hd % 2], "ET%d" % (hd % 2)
            bd = self.nb()
            for mc in range(2):
                S.op("pe", lambda e, mc=mc: e.matmul(PS[:, bd, 0:T], lhsT=self.onesb[:], rhs=ET[:, mc, :], start=(mc == 0), stop=(mc == 1)),
                     reads=[(ek, mc), "onesb"], writes=[("ps", bd)])
            r, rk = self.rs[hd % 2], ("rs", hd % 2)
            S.op("dve", lambda e: e.reciprocal(out=r[:], in_=PS[:, bd, 0:T]), reads=[("ps", bd)], writes=[rk])
            for dd in range(2):
                b = self.nb()
                for mc in range(2):
                    S.op("pe", lambda e, mc=mc, dd=dd, b=b: e.matmul(PS[:, b, 0:T], lhsT=V[:, mc, (2 * hd + dd) * 128:(2 * hd + dd + 1) * 128], rhs=ET[:, mc, :],
                                                                      start=(mc == 0), stop=(mc == 1)),
                         reads=[("V", mc), (ek, mc)], writes=[("ps", b)])
                S.op("dve", lambda e, dd=dd, b=b: e.tensor_tensor(out=W1[:, 8 + 2 * hd + dd, :], in0=PS[:, b, 0:T], in1=r[:], op=ALU.mult),
                     reads=[("ps", b), rk], writes=[("W1", 8 + 2 * hd + dd)])

        att_s1(0)
        for hd in range(4):
            if hd + 1 < 4:
                att_s1(hd + 1)
            att_s2(hd)
        OT = W1[:, 8:16, :]
        t_keys = None
        for hf in range(2):
            t, key = self.wslab("w_o_x", 0, 8, hf * 512, 512)
            for mm in range(4):
                b = self.nb()
                for k in range(8):
                    S.op("pe", lambda e, t=t, mm=mm, k=k, b=b: e.matmul(PS[:, b, 0:T], lhsT=t[:, k, mm * 128:(mm + 1) * 128], rhs=W1[:, 8 + k, :],
                                                                         start=(k == 0), stop=(k == 7)), reads=[key, ("W1", 8 + k)], writes=[("ps", b)])
                self.resid_add(hf * 4 + mm, b, 1.0, T)
        self.dump("h3", h[:], [("h", c) for c in range(8)], [128, 8, T])
        self.ffn("ffn2")
        if ti + 1 < self.NT:
            self.issue_load(("x", s, ti + 1), self.x_d[s, t0 + T:t0 + 2 * T, :], T)
        elif s + 1 < self.NSEQ:
            self.issue_load(("mem", s + 1), self.mem_d[s + 1], MEM)
        io = self.io
        junk = self.tf[0][:].bitcast(BF16)
        ssq, ssum = self.ssq, self.ssum
        for q in range(QT):
            banks = [self.nb(), self.nb()]
            for hf in range(2):
                b = banks[hf]
                for cc in range(4):
                    c = hf * 4 + cc
                    S.op("pe", lambda e, c=c, cc=cc, q=q, b=b: e.transpose(PS[:, b, cc * 128:(cc + 1) * 128], h[:, c, q * 128:(q + 1) * 128], self.idf[:]),
                         reads=[("h", c), "idf"], writes=[("ps", b)])
            for hf in range(2):
                S.op("act", lambda e, q=q, hf=hf, b=banks[hf]: e.activation(out=junk[:, 0:512], in_=PS[:, b, :], func=AF.Square, accum_out=ssq[:, 2 * q + hf:2 * q + hf + 1]),
                     reads=[("ps", banks[hf])], writes=[("tf", 0), ("ssq", 2 * q + hf)])
            S.op("dve", lambda e, q=q: e.tensor_tensor(out=ssum[:, q:q + 1], in0=ssq[:, 2 * q:2 * q + 1], in1=ssq[:, 2 * q + 1:2 * q + 2], op=ALU.add),
                 reads=[("ssq", 2 * q), ("ssq", 2 * q + 1)], writes=[("ssum", q)])
            S.op("act", lambda e, q=q: e.activation(out=ssum[:, q:q + 1], in_=ssum[:, q:q + 1], func=AF.Ln, bias=self.epsb[:, 0:1], scale=1.0 / D),
                 reads=[("ssum", q), "epsb"], writes=[("ssum", q)])
            S.op("act", lambda e, q=q: e.activation(out=ssum[:, q:q + 1], in_=ssum[:, q:q + 1], func=AF.Exp, scale=-0.5), reads=[("ssum", q)], writes=[("ssum", q)])
            for hf in range(2):
                S.op("dve", lambda e, q=q, hf=hf, b=banks[hf]: e.scalar_tensor_tensor(
                    out=io[:, q, hf * 512:(hf + 1) * 512], in0=PS[:, b, :], scalar=ssum[:, q:q + 1], in1=cp[:, CP_GFB + hf * 512:CP_GFB + (hf + 1) * 512],
                    op0=ALU.mult, op1=ALU.mult), reads=[("ps", banks[hf]), ("ssum", q), "cp"], writes=self.iok(q))
        dst = self.y_d[s, t0:t0 + T, :].rearrange("(q p) d -> p q d", p=128)
        S.dma("pool", lambda e: e.dma_start(out=dst, in_=io[:, 0:QT, :]), reads=[k for q in range(QT) for k in self.iok(q)], writes=["y_out"], key="yout")

    def ssd_tile(self):
        QT = self.QT
        items = [(q, g) for q in range(QT) for g in range(4)]
        asteps = [16 // len(items) + (1 if i < 16 % len(items) else 0) for i in range(len(items))]
        self.ssd_R(*items[0])
        self.ssd_R(*items[1])
        for i, (q, g) in enumerate(items):
            if g == 0:
                self.ssd_pre(q)
            self.ssd_A(q, g)
            for _ in range(asteps[i]):
                next(self.agen, None)
            if i + 2 < len(items):
                self.ssd_R(*items[i + 2])
            if g >= 1:
                self.ssd_B(q, g - 1)
            if g == 3:
                if q == QT - 1:
                    for _ in self.agen:
                        pass
                    self.group_norm(0)
                self.ssd_B(q, 3)

    def group_norm(self, g):
        S, PS, T = self.S, self.PS, self.T
        W2, W3, cp = self.W2, self.W3, self.cp
        S.op("act", lambda e: e.activation(out=W2[:, 4 * g:4 * g + 4, 0:T], in_=W3[:, 4 * g:4 * g + 4, :], func=AF.Square),
             reads=[("W3", 4 * g + i) for i in range(4)], writes=[("W2", 4 * g + i) for i in range(4)])
        b = self.nb()
        for i in range(4):
            S.op("pe", lambda e, i=i: e.matmul(PS[:, b, 0:T], lhsT=self.onesb[:], rhs=W2[:, 4 * g + i, 0:T], start=(i == 0), stop=(i == 3)),
                 reads=[("W2", 4 * g + i), "onesb"], writes=[("ps", b)])
        r, rk = self.rs[g % 2], ("rs", g % 2)
        S.op("act", lambda e: e.activation(out=r[:], in_=PS[:, b, 0:T], func=AF.Ln, bias=self.epsb[:, 0:1], scale=1.0 / 512),
             reads=[("ps", b), "epsb"], writes=[rk])
        S.op("act", lambda e: e.activation(out=r[:], in_=r[:], func=AF.Exp, scale=-0.5), reads=[rk], writes=[rk])
        for i in range(4):
            c = 4 * g + i
            S.op("dve", lambda e, c=c: e.scalar_tensor_tensor(out=W3[:, c, :], in0=W3[:, c, :], scalar=cp[:, CP_GN + c:CP_GN + c + 1], in1=r[:],
                                                              op0=ALU.mult, op1=ALU.mult), reads=[("W3", c), rk, "cp"], writes=[("W3", c)])

    def ssd_R(self, q, g):
        gi = g % 2
        R = self.R[gi]
        if g % 2 == 0:
            self.S.op("pool", lambda e: e.tensor_tensor(out=R[:], in0=self.lat[:, q, g * 8:(g + 1) * 8].unsqueeze(2).to_broadcast([128, 8, 128]),
                                                        in1=self.Uf[:].unsqueeze(1).to_broadcast([128, 8, 128]), op=ALU.mult),
                      reads=["lat", "Uf"], writes=[("R", gi)])
        else:
            for h in range(8):
                self.S.op("act", lambda e, h=h: e.activation(out=R[:, h, :], in_=self.Uf[:], func=AF.Copy, scale=self.lat[:, q, g * 8 + h:g * 8 + h + 1]),
                          reads=["lat", "Uf"], writes=[("R", gi)])

    def ssd_pre(self, q):
        S, PS = self.S, self.PS
        W2 = self.W2
        c0 = q * 128
        lat, dtt = self.lat, self.dtt
        ex, exk = self.exs[q % 2], ("ex", q % 2)
        b = self.nb()
        for i, L in enumerate((self.Uf, self.Lf, self.onesf)):
            S.op("pe", lambda e, i=i, L=L, b=b: e.matmul(PS[:, b, i * 32:(i + 1) * 32], lhsT=L[:], rhs=lat[:, q, :], start=True, stop=True),
                 reads=["lat", "Uf", "Lf", "onesf"], writes=[("ps", b)])
        S.op("act", lambda e: e.activation(out=ex[:].rearrange("p a h -> p (a h)"), in_=PS[:, b, 0:96], func=AF.Exp), reads=[("ps", b)], writes=[exk])
        xdt, xdte, Btm, CBm = self.xdt, self.xdte, self.Btm, self.CBm
        dd, ddk = self.dds[q % 2], ("dd", q % 2)
        S.op("dve", lambda e: e.tensor_tensor(out=dd[:], in0=dtt[:, q, :], in1=ex[:, 1, :], op=ALU.mult), reads=["dtt", exk], writes=[ddk])
        for half in range(2):
            bb = self.nb()
            pbf = PS[:, bb, :].bitcast(BF16)
            for cc in range(8):
                c = half * 8 + cc
                S.op("pe", lambda e, c=c, cc=cc, pbf=pbf: e.transpose(pbf[:, cc * 128:(cc + 1) * 128], W2[:, c, c0:c0 + 128], self.idb[:]),
                     reads=[("W2", c), "idb"], writes=[("ps", bb)])
            S.op("dve", lambda e, half=half, pbf=pbf: e.tensor_tensor(
                out=xdt[:, half * 16:(half + 1) * 16, :], in0=pbf[:, 0:1024].rearrange("p (h d) -> p h d", d=64),
                in1=dtt[:, q, half * 16:(half + 1) * 16].unsqueeze(2).to_broadcast([128, 16, 64]), op=ALU.mult),
                reads=[("ps", bb), "dtt"], writes=[("xdt", half)])
            S.op("dve", lambda e, half=half, pbf=pbf: e.tensor_tensor(
                out=xdte[:, half * 16:(half + 1) * 16, :], in0=pbf[:, 0:1024].rearrange("p (h d) -> p h d", d=64),
                in1=dd[:, half * 16:(half + 1) * 16].unsqueeze(2).to_broadcast([128, 16, 64]), op=ALU.mult),
                reads=[("ps", bb), ddk], writes=[("xdte", half)])
        bb = self.nb()
        pbf2 = PS[:, bb, :].bitcast(BF16)
        for g in range(4):
            S.op("pe", lambda e, g=g: e.transpose(pbf2[:, g * 128:(g + 1) * 128], W2[:, 16 + g, c0:c0 + 128], self.idb[:]),
                 reads=[("W2", 16 + g), "idb"], writes=[("ps", bb)])
        S.op("act", lambda e: e.activation(out=Btm[:].rearrange("p g n -> p (g n)"), in_=pbf2[:, 0:512], func=AF.Copy), reads=[("ps", bb)], writes=["Btm"])
        bc = self.nb()
        for g in range(4):
            S.op("pe", lambda e, g=g: e.matmul(PS[:, bc, g * 128:(g + 1) * 128], lhsT=W2[:, 16 + g, c0:c0 + 128], rhs=W2[:, 20 + g, c0:c0 + 128], start=True, stop=True),
                 reads=[("W2", 16 + g), ("W2", 20 + g)], writes=[("ps", bc)])
        S.op("dve", lambda e: e.tensor_tensor(out=CBm[:], in0=PS[:, bc, :].rearrange("p (g l) -> p g l", l=128),
                                              in1=self.Uf[:].unsqueeze(1).to_broadcast([128, 4, 128]), op=ALU.mult),
             reads=[("ps", bc), "Uf"], writes=["CBm"])

    def ssd_A(self, q, g):
        S, PS = self.S, self.PS
        W2 = self.W2
        c0 = q * 128
        lat = self.lat
        ex, exk = self.exs[q % 2], ("ex", q % 2)
        gi = g % 2
        R, E, MT, yo = self.R[gi], self.E[gi], self.MT[gi], self.yo[gi]
        Rk, Ek, Mk, yk = ("R", gi), ("E", gi), ("MT", gi), ("yo", gi)
        bo = self.nb()
        S.op("pe", lambda e: e.matmul(PS[:, bo, :], lhsT=W2[:, 20 + g, c0:c0 + 128], rhs=self.prevb[:, g * 8:(g + 1) * 8, :].rearrange("p h d -> p (h d)"),
                                      start=True, stop=True), reads=[("W2", 20 + g), ("prevb", g)], writes=[("ps", bo)])
        S.op("dve", lambda e: e.tensor_tensor(out=yo[:], in0=PS[:, bo, :].rearrange("p (h d) -> p h d", d=64),
                                              in1=ex[:, 0, g * 8:(g + 1) * 8].unsqueeze(2).to_broadcast([128, 8, 64]), op=ALU.mult),
             reads=[("ps", bo), exk], writes=[yk])
        for hh2 in range(2):
            bs = self.nb()
            S.op("pe", lambda e, hh2=hh2, bs=bs: e.matmul(PS[:, bs, :], lhsT=self.Lb[:], rhs=R[:, hh2 * 4:(hh2 + 1) * 4, :].rearrange("p h l -> p (h l)"),
                                                          start=True, stop=True), reads=[Rk, "Lb"], writes=[("ps", bs)])
            S.op("act", lambda e, hh2=hh2, bs=bs: e.activation(out=E[:, hh2 * 4:(hh2 + 1) * 4, :].rearrange("p h l -> p (h l)"), in_=PS[:, bs, :], func=AF.Exp),
                 reads=[("ps", bs)], writes=[Ek])
        S.op("dve", lambda e: e.tensor_tensor(out=MT[:], in0=E[:], in1=self.CBm[:, g, :].unsqueeze(1).to_broadcast([128, 8, 128]), op=ALU.mult),
             reads=[Ek, "CBm"], writes=[Mk])

    def ssd_B(self, q, g):
        S, PS = self.S, self.PS
        W1, W2, W3 = self.W1, self.W2, self.W3
        c0 = q * 128
        ex, exk = self.exs[q % 2], ("ex", q % 2)
        gi = g % 2
        MT, yo = self.MT[gi], self.yo[gi]
        Mk, yk = ("MT", gi), ("yo", gi)
        xdt, xdte, Btm = self.xdt, self.xdte, self.Btm
        by = self.nb()
        for pr in range(4):
            cch = g * 4 + pr
            cols = slice(pr * 128, (pr + 1) * 128)
            S.op("pe", lambda e, pr=pr, cols=cols: e.matmul(PS[:, by, cols], lhsT=yo[:, 2 * pr:2 * pr + 2, :].rearrange("p h d -> p (h d)"), rhs=self.idb[:],
                                                            start=True, stop=False), reads=[yk, "idb"], writes=[("ps", by)])
            S.op("pe", lambda e, cch=cch, cols=cols: e.matmul(PS[:, by, cols], lhsT=self.diagD[:, cch, :], rhs=W2[:, cch, c0:c0 + 128], start=False, stop=False),
                 reads=["diagD", ("W2", cch)], writes=[("ps", by)])
            for hx in range(2):
                hh = 2 * pr + hx
                hglob = g * 8 + hh
                S.op("pe", lambda e, hx=hx, hh=hh, hglob=hglob, cols=cols: e.matmul(
                    PS[hx * 64:(hx + 1) * 64, by, cols], lhsT=xdt[:, hglob, :], rhs=MT[:, hh, :], start=False, stop=(hx == 1)),
                    reads=[("xdt", hglob // 16), Mk], writes=[("ps", by)])
        S.op("dve", lambda e: e.tensor_tensor(out=W3[:, 4 * g:4 * g + 4, c0:c0 + 128], in0=PS[:, by, :].rearrange("p (c l) -> p c l", l=128),
                                              in1=W1[:, 4 * g:4 * g + 4, c0:c0 + 128], op=ALU.mult),
             reads=[("ps", by)] + [("W1", 4 * g + i) for i in range(4)], writes=[("W3", 4 * g + i) for i in range(4)])
        bst = self.nb()
        S.op("pe", lambda e: e.matmul(PS[:, bst, :], lhsT=Btm[:, g, :], rhs=xdte[:, g * 8:(g + 1) * 8, :].rearrange("p h d -> p (h d)"), start=True, stop=True),
             reads=["Btm", ("xdte", g // 2)], writes=[("ps", bst)])
        stv = self.state[:, g * 8:(g + 1) * 8, :]
        stk = ("state", g)
        S.op("dve", lambda e: e.tensor_tensor(out=stv, in0=stv, in1=ex[:, 2, g * 8:(g + 1) * 8].unsqueeze(2).to_broadcast([128, 8, 64]), op=ALU.mult),
             reads=[stk, exk], writes=[stk])
        S.op("dve", lambda e: e.tensor_tensor(out=stv, in0=stv, in1=PS[:, bst, :].rearrange("p (h d) -> p h d", d=64), op=ALU.add),
             reads=[stk, ("ps", bst)], writes=[stk])
        S.op("pool", lambda e: e.tensor_copy(out=self.prevb[:, g * 8:(g + 1) * 8, :], in_=stv), reads=[stk], writes=[("prevb", g)])


def _build_once(NSEQ, NT, T, dbg, cast_order, dry):
    nc = bass.Bass("TRN2", target_bir_lowering=False)
    st = ExitStack()
    B = Builder(nc, st, NSEQ, NT, T, dbg)
    B.cast_order, B.dry = cast_order, dry
    B.epsb = B.sb("epsb", [128, 1], F32)
    B.oneb = B.sb("oneb", [128, 1], F32)
    B.S.op("pool", lambda e: e.memset(B.epsb[:], EPS), writes=["epsb"])
    B.S.op("pool", lambda e: e.memset(B.oneb[:], 1.0), writes=["oneb"])
    B.build()
    st.close()
    return nc, B


def build_program(NSEQ=4, NT=4, T=512, dbg=()):
    _, B0 = _build_once(1, 1, T, (), None, True)
    return _build_once(NSEQ, NT, T, dbg, list(B0.slab_use), False)


def pack_consts(inp):
    cp = np.zeros((128, NCP), np.float32)
    def cols(v):
        v = np.asarray(v, np.float32).reshape(-1, 128)
        return v.T
    for nm, key in (("ffn1", "ffn1_norm"), ("mix", "mix_norm"), ("xattn", "xattn_norm"), ("mem", "mem_norm"), ("ffn2", "ffn2_norm")):
        cp[:, CP_G[nm]:CP_G[nm] + 8] = cols(inp[key][0])
    cp[:, CP_G["final"]:CP_G["final"] + 8] = cols(inp["final_norm"])
    for k in range(3):
        cp[:, CP_CAW + k * 8:CP_CAW + (k + 1) * 8] = cols(inp["conv_a_w"][0, k])
    for k in range(4):
        cp[:, CP_SCW + k * 24:CP_SCW + (k + 1) * 24] = cols(inp["ssm_conv_w"][0, k])
    cp[:, CP_SCB:CP_SCB + 24] = cols(inp["ssm_conv_b"][0])
    cp[:, CP_GN:CP_GN + 16] = cols(inp["ssm_norm"][0])
    cp[:, CP_D:CP_D + 16] = cols(np.repeat(np.asarray(inp["ssm_d"][0], np.float32), 64))
    cp[:, CP_DTB:CP_DTB + 32] = np.broadcast_to(np.asarray(inp["ssm_dt_bias"][0], np.float32), (128, 32))
    cp[:, CP_ALOG:CP_ALOG + 32] = np.broadcast_to(np.asarray(inp["ssm_a_log"][0], np.float32), (128, 32))
    cp[:, CP_GFB:CP_GFB + 1024] = np.broadcast_to(np.asarray(inp["final_norm"], np.float32), (128, 1024))
    return cp


_CACHE = {}


def kernel(**inputs):
    inp = {k: np.asarray(v) for k, v in inputs.items()}
    NCORES, NSEQ, T = 8, 4, 512
    NT = SEQ // T
    if "nc" not in _CACHE:
        _CACHE["nc"] = build_program(NSEQ, NT, T)[0]
    nc = _CACHE["nc"]
    cp = pack_consts(inp)
    ws = {n: np.ascontiguousarray(inp[n][0], dtype=np.float32) for n, _, _ in BIGW}
    x = np.ascontiguousarray(inp["x"], dtype=np.float32)
    mem = np.ascontiguousarray(inp["mem"], dtype=np.float32)
    in_maps = []
    for c in range(NCORES):
        m = {"x": x[c * NSEQ:(c + 1) * NSEQ], "mem": mem[c * NSEQ:(c + 1) * NSEQ], "cpk": cp}
        m.update(ws)
        in_maps.append(m)
    res = run_bass_kernel_spmd(nc, in_maps, core_ids=list(range(NCORES)))
    out = np.concatenate([np.asarray(r["y"]) for r in res.results], axis=0)
    return out.astype(np.float32, copy=False)
```

```python
import numpy as np
from contextlib import ExitStack
import concourse.bass as bass
import concourse.mybir as mybir
from concourse.bass_utils import run_bass_kernel_spmd

F32 = mybir.dt.float32
BF16 = mybir.dt.bfloat16
ALU = mybir.AluOpType
AF = mybir.ActivationFunctionType

D = 1024
SEQ = 2048
MEM = 256
DFF = 2816
NJ = DFF // 128
DIN = 10272
EPS = 1e-6
C_AB, C_AC, C_AV, C_Z, C_XBC, C_DT, C_GA, C_GB = 0, 1024, 2048, 3072, 5120, 8192, 8224, 9248

CP_G = {"ffn1": 0, "mix": 8, "xattn": 16, "mem": 24, "ffn2": 32, "final": 40}
CP_CAW = 48
CP_SCW = 72
CP_SCB = 168
CP_GN = 192
CP_D = 208
CP_DTB = 224
CP_ALOG = 256
CP_GFB = 288
NCP = 288 + 1024


class Op:
    __slots__ = ("eng", "fn", "reads", "writes", "dma", "deps", "token", "waits", "idx", "n_inc", "has_dep")

    def __init__(self, eng, fn, reads, writes, dma):
        self.eng = eng
        self.fn = fn
        self.reads = tuple(reads)
        self.writes = tuple(writes)
        self.dma = dma
        self.deps = ()
        self.token = None
        self.waits = ()
        self.has_dep = False
        self.n_inc = 1


class Sched:
    ENGS = ("pe", "act", "dve", "pool", "sp")

    def __init__(self, nc, stack):
        self.nc = nc
        self.stack = stack
        self.ops = []
        self.sems = {}

    def sem(self, key):
        if key not in self.sems:
            self.sems[key] = self.stack.enter_context(self.nc.semaphore("s%d" % len(self.sems)))
        return self.sems[key]

    def op(self, eng, fn, reads=(), writes=(), dma=None):
        o = Op(eng, fn, reads, writes, dma)
        self.ops.append(o)
        return o

    def dma(self, eng, fn, reads=(), writes=(), key="d", n=1):
        o = self.op(eng, fn, reads, writes, dma=key)
        o.n_inc = n
        return o

    def finalize(self):
        ops = self.ops
        last_writer = {}
        readers = {}
        for i, o in enumerate(ops):
            o.idx = i
            deps = set()
            for k in o.reads:
                w = last_writer.get(k)
                if w is not None:
                    deps.add(w)
            for k in o.writes:
                w = last_writer.get(k)
                if w is not None:
                    deps.add(w)
                rl = readers.get(k)
                if rl:
                    deps.update(rl)
            deps.discard(i)
            for k in o.reads:
                readers.setdefault(k, []).append(i)
            for k in o.writes:
                last_writer[k] = i
                readers[k] = []
            fd = []
            for d in deps:
                p = ops[d]
                if p.dma is None and o.dma is None and p.eng == "pe" and o.eng == "pe":
                    continue
                fd.append(d)
                p.has_dep = True
            o.deps = fd
        cnt = {e: 0 for e in self.ENGS}
        dcnt = {}
        for o in ops:
            if o.dma is not None:
                dcnt[o.dma] = dcnt.get(o.dma, 0) + 16 * o.n_inc
                o.token = (("dma", o.dma), dcnt[o.dma])
            elif o.has_dep:
                cnt[o.eng] += 1
                o.token = (("eng", o.eng), cnt[o.eng])
        waited = {e: {} for e in self.ENGS}
        for o in ops:
            need = {}
            for d in o.deps:
                sk, v = ops[d].token
                if need.get(sk, 0) < v:
                    need[sk] = v
            w = waited[o.eng]
            ws = []
            for sk, v in need.items():
                if w.get(sk, 0) < v:
                    w[sk] = v
                    ws.append((sk, v))
            o.waits = ws

    def emit(self):
        nc = self.nc
        self.finalize()
        for e in self.ENGS:
            self.sem(("eng", e))
        for o in self.ops:
            if o.dma is not None:
                self.sem(("dma", o.dma))
        by_eng = {e: [o for o in self.ops if o.eng == e] for e in self.ENGS}
        sems = self.sems

        def run(eng_name, eng):
            mysem = sems[("eng", eng_name)]
            for o in by_eng[eng_name]:
                for sk, v in o.waits:
                    eng.wait_ge(sems[sk], v)
                ins = o.fn(eng)
                if o.dma is not None:
                    if not isinstance(ins, (list, tuple)):
                        ins = [ins]
                    assert len(ins) == o.n_inc, (len(ins), o.n_inc)
                    s = sems[("dma", o.dma)]
                    for x in ins:
                        x.then_inc(s, 16)
                elif o.token is not None:
                    ins.then_inc(mysem, 1)

        with nc.Block() as block:
            @block.tensor
            def _(e):
                run("pe", e)

            @block.scalar
            def _(e):
                run("act", e)

            @block.vector
            def _(e):
                run("dve", e)

            @block.gpsimd
            def _(e):
                run("pool", e)

            @block.sync
            def _(e):
                run("sp", e)


BIGW = [("ffn1_w_gate_up", D, 2 * DFF), ("ffn1_w_down", DFF, D), ("w_in", D, DIN), ("w_out_a", D, D),
        ("w_out_ssm", 2 * D, D), ("w_mix_out", D, D), ("w_q", D, D), ("w_kv", D, 2 * D), ("w_o_x", D, D),
        ("ffn2_w_gate_up", D, 2 * DFF), ("ffn2_w_down", DFF, D)]


CAST_ORDER = ["w_kv", "ffn1_w_gate_up", "ffn1_w_down", "w_in", "w_out_ssm", "w_out_a", "w_mix_out", "w_q", "w_o_x", "ffn2_w_gate_up", "ffn2_w_down"]


class Builder:
    def __init__(self, nc, st, NSEQ, NT, T, dbg=()):
        self.nc, self.st, self.NSEQ, self.NT, self.T = nc, st, NSEQ, NT, T
        self.QT = T // 128
        self.S = Sched(nc, st)
        self.dbg = set(dbg)
        self.dbg_out = {}
        self.psi = 0
        self.wsi = 0
        self.ev = 0
        self.slab_use = []
        self.cast_order = None
        self.dry = False

    def sb(self, name, shape, dt):
        return self.st.enter_context(self.nc.sbuf_tensor(name, shape, dt))

    def cvk(self, m):
        return [("cv", m), ("raw", 0), ("raw", 1)] + [("ta", j) for j in range(4)]

    def iok(self, q):
        return [("W3", q * self.cpq + i) for i in range(self.cpq)]

    def nb(self):
        b = self.psi % 8
        self.psi += 1
        return b

    def dump(self, name, ap, keys, shape):
        if name not in self.dbg:
            return
        d = self.nc.dram_tensor("dbg_" + name, list(shape), F32, kind="ExternalOutput").ap()
        stg = self.sb("dbgs_" + name, list(shape), F32)
        self.S.op("dve", lambda e: e.tensor_copy(out=stg[:], in_=ap), reads=keys, writes=["dbgs_" + name])
        self.S.dma("pool", lambda e: e.dma_start(out=d, in_=stg[:]), reads=["dbgs_" + name], writes=["dbgd_" + name], key="dbg")
        self.dbg_out[name] = "dbgd_" + name

    def wslab(self, wname, r0, nkc, c0, ncols):
        i = self.wsi % self.NWS
        self.wsi += 1
        t = self.ws[i]
        key = ("ws", i)
        for blk in range(c0 // 512, (c0 + ncols - 1) // 512 + 1):
            if (wname, blk) not in self.slab_use:
                self.slab_use.append((wname, blk))
        src = self.wbf[wname][r0 * 128:(r0 + nkc) * 128, c0:c0 + ncols].rearrange("(k p) n -> p k n", p=128)
        dst = t[:, 0:nkc, 0:ncols]
        self.S.dma("sp", lambda e: e.dma_start(out=dst, in_=src), reads=[("wbf", wname, blk) for blk in range(c0 // 512, (c0 + ncols - 1) // 512 + 1)], writes=[key], key=("w", i))
        return t, key

    def build(self):
        nc, S, T, QT = self.nc, self.S, self.T, self.QT
        NSEQ, NT = self.NSEQ, self.NT
        sb = self.sb
        dram = lambda n, s, dt, kind: nc.dram_tensor(n, s, dt, kind=kind).ap()
        self.x_d = dram("x", [NSEQ, SEQ, D], F32, "ExternalInput")
        self.mem_d = dram("mem", [NSEQ, MEM, D], F32, "ExternalInput")
        self.cpk_d = dram("cpk", [128, NCP], F32, "ExternalInput")
        self.y_d = dram("y", [NSEQ, NT * T, D], F32, "ExternalOutput")
        self.wf32 = {n: dram(n, [k, m], F32, "ExternalInput") for n, k, m in BIGW}
        self.wbf = {n: dram(n + "_bf", [k, m], BF16, "Internal") for n, k, m in BIGW}

        self.PS = self.st.enter_context(nc.psum_tensor("PS", [128, 8, 512], F32))
        PS = self.PS
        self.NWS = 4
        self.ws = [sb("ws%d" % i, [128, 8, 512], BF16) for i in range(self.NWS)]
        self.cp = sb("cp", [128, NCP], F32)
        self.idf = sb("idf", [128, 128], F32)
        self.idb = sb("idb", [128, 128], BF16)
        self.onesf = sb("onesf", [128, 128], F32)
        self.onesb = sb("onesb", [128, 128], BF16)
        self.Uf = sb("Uf", [128, 128], F32)
        self.Lf = sb("Lf", [128, 128], F32)
        self.Lb = sb("Lb", [128, 128], BF16)
        self.Ub = sb("Ub", [128, 128], BF16)
        self.diagD = sb("diagD", [128, 16, 128], BF16)
        self.aneg = sb("aneg", [128, 32], F32)
        self.io = sb("io", [128, QT, D], F32)
        self.h = sb("h", [128, 8, T], F32)
        self.u = sb("u", [128, 8, T], BF16)
        self.W2 = sb("W2", [128, 24, T + 4], BF16)
        self.W1 = sb("W1", [128, NJ, T], BF16)
        self.W3 = self.io[:].bitcast(BF16).rearrange("p q (c n) -> p (q c) n", n=T)
        self.cpq = 16 // QT
        self.xin = self.W1[:, 0:16, :].bitcast(F32).rearrange("p (q c) n -> p q (c n)", q=QT)
        self.loaded = None
        self.rs = [sb("rs%d" % i, [128, T], F32) for i in range(2)]
        self.tf = [sb("tf%d" % i, [128, T], F32) for i in range(2)]
        self.arena = sb("arena", [128, 8 * (T + 2)], BF16)
        rw = 2 * (T + 4)
        self.raw = [self.arena[:, i * rw:(i + 1) * rw].bitcast(F32) for i in range(2)]
        self.cv = self.arena[:, :].rearrange("p (c n) -> p c n", n=T + 2)
        self.histx = sb("histx", [128, 24, 3], F32)
        self.hista = sb("hista", [128, 8, 2], BF16)
        self.tab = [self.arena[:, 4 * (T + 4):4 * (T + 4) + 4 * T].rearrange("p (c n) -> p c n", n=T)] * 2
        self.mg = sb("mg", [128, 4, T], F32)
        self.dtt = sb("dtt", [128, QT, 32], F32)
        self.kmax2 = sb("kmax2", [128, 4], F32)
        self.negc = sb("negc", [128, 4], F32)
        self.ssq = sb("ssq", [128, 2 * QT], F32)
        self.ssum = sb("ssum", [128, QT], F32)
        self.lat = sb("lat", [128, QT, 32], F32)
        self.latb = sb("latb", [128, QT, 32], BF16)
        self.sp1 = sb("sp1", [128, QT, 32], F32)
        self.sp2 = sb("sp2", [128, QT, 32], F32)
        self.exs = [sb("ex%d" % i, [128, 3, 32], F32) for i in range(2)]
        self.dds = [sb("dd%d" % i, [128, 32], F32) for i in range(2)]
        self.R = [sb("R%d" % i, [128, 8, 128], BF16) for i in range(2)]
        self.E = [sb("E%d" % i, [128, 8, 128], BF16) for i in range(2)]
        self.MT = [sb("MT%d" % i, [128, 8, 128], BF16) for i in range(2)]
        self.xdt = sb("xdt", [128, 32, 64], BF16)
        self.xdte = sb("xdte", [128, 32, 64], BF16)
        self.Btm = sb("Btm", [128, 4, 128], BF16)
        self.CBm = sb("CBm", [128, 4, 128], BF16)
        self.yo = [sb("yo%d" % i, [128, 8, 64], BF16) for i in range(2)]
        self.state = sb("state", [128, 32, 64], F32)
        self.prevb = sb("prevb", [128, 32, 64], BF16)
        self.KT = sb("KT", [128, 8, MEM], BF16)
        self.V = sb("V", [128, 2, D], BF16)
        self.ET = [sb("ET%d" % i, [128, 2, T], BF16) for i in range(2)]

        self.setup()
        for s in range(NSEQ):
            self.seq_setup(s)
            for t in range(NT):
                self.tile(s, t)
        S.op("pool", lambda e: e.nop(), reads=["y_out"] + list(self.dbg_out.values()))
        if not self.dry:
            S.emit()

    def setup(self):
        S, nc = self.S, self.nc
        cp = self.cp
        S.dma("pool", lambda e: e.dma_start(out=cp[:], in_=self.cpk_d), writes=["cp"], key="cp")
        idf, idb, onesf, onesb, Uf, Lf = self.idf, self.idb, self.onesf, self.onesb, self.Uf, self.Lf
        S.op("pool", lambda e: e.memset(idf[:], 0.0), writes=["idf"])
        S.op("pool", lambda e: e.affine_select(out=idf[:], in_=idf[:], compare_op=ALU.not_equal, fill=1.0, base=0,
                                                pattern=[[-1, 128]], channel_multiplier=1), reads=["idf"], writes=["idf"])
        S.op("pool", lambda e: e.memset(onesf[:], 1.0), writes=["onesf"])
        S.op("pool", lambda e: e.memset(onesb[:], 1.0), writes=["onesb"])
        S.op("pool", lambda e: e.affine_select(out=Uf[:], in_=onesf[:], compare_op=ALU.is_ge, fill=0.0, base=0,
                                                pattern=[[1, 128]], channel_multiplier=-1), reads=["onesf"], writes=["Uf"])
        S.op("pool", lambda e: e.affine_select(out=Lf[:], in_=onesf[:], compare_op=ALU.is_gt, fill=0.0, base=0,
                                                pattern=[[-1, 128]], channel_multiplier=1), reads=["onesf"], writes=["Lf"])
        S.op("dve", lambda e: e.tensor_copy(out=idb[:], in_=idf[:]), reads=["idf"], writes=["idb"])
        S.op("dve", lambda e: e.tensor_copy(out=self.Lb[:], in_=Lf[:]), reads=["Lf"], writes=["Lb"])
        S.op("dve", lambda e: e.tensor_copy(out=self.Ub[:], in_=Uf[:]), reads=["Uf"], writes=["Ub"])
        for c in range(16):
            S.op("dve", lambda e, c=c: e.tensor_scalar(out=self.diagD[:, c, :], in0=idf[:], scalar1=cp[:, CP_D + c:CP_D + c + 1],
                                                        scalar2=None, op0=ALU.mult), reads=["idf", "cp"], writes=["diagD"])
        S.op("act", lambda e: e.activation(out=self.aneg[:], in_=cp[:, CP_ALOG:CP_ALOG + 32], func=AF.Exp), reads=["cp"], writes=["aneg"])
        S.op("dve", lambda e: e.tensor_scalar(out=self.aneg[:], in0=self.aneg[:], scalar1=-1.0, scalar2=None, op0=ALU.mult),
             reads=["aneg"], writes=["aneg"])
        self.issue_load(("mem", 0), self.mem_d[0], MEM)
        dims = {n: (k, m) for n, k, m in BIGW}
        order = list(self.cast_order or [])
        for n, k, m in BIGW:
            for blk in range((m + 511) // 512):
                if (n, blk) not in order:
                    order.append((n, blk))
        for n, blk in order:
            m = dims[n][1]
            src, dst = self.wf32[n], self.wbf[n]
            c0, c1 = blk * 512, min(m, blk * 512 + 512)
            S.dma("pool", lambda e, src=src, dst=dst, c0=c0, c1=c1: e.dma_start(out=dst[:, c0:c1], in_=src[:, c0:c1]),
                  writes=[("wbf", n, blk)], key=("cast", n, blk))

    def evac_eng(self):
        self.ev += 1
        return "act" if self.ev % 2 else "dve"

    def copy(self, eng, out, in_, reads, writes, scale=None):
        if eng == "act":
            if scale is None:
                self.S.op("act", lambda e: e.activation(out=out, in_=in_, func=AF.Copy), reads=reads, writes=writes)
            else:
                self.S.op("act", lambda e: e.activation(out=out, in_=in_, func=AF.Copy, scale=scale), reads=reads, writes=writes)
        else:
            if scale is None:
                self.S.op(eng, lambda e: e.tensor_copy(out=out, in_=in_), reads=reads, writes=writes)
            else:
                self.S.op(eng, lambda e: e.tensor_scalar(out=out, in0=in_, scalar1=scale, scalar2=None, op0=ALU.mult), reads=reads, writes=writes)

    def xk(self, q):
        return [("W1", q * self.cpq + i) for i in range(self.cpq)]

    def issue_load(self, tag, src_rows, nrows):
        nq = nrows // 128
        xin = self.xin
        self.S.dma("act", lambda e: e.dma_start(out=xin[:, 0:nq, :], in_=src_rows.rearrange("(q p) d -> p q d", p=128)),
                   writes=[k for q in range(nq) for k in self.xk(q)], key="io")
        self.loaded = tag

    def load_fm(self, tag, src_rows, nrows, dst, dst_key, ncols_off=0):
        S, PS, xin = self.S, self.PS, self.xin
        nq = nrows // 128
        if self.loaded != tag:
            self.issue_load(tag, src_rows, nrows)
        for c in range(8):
            b = self.nb()
            for q in range(nq):
                S.op("pe", lambda e, c=c, q=q, b=b: e.transpose(PS[:, b, q * 128:(q + 1) * 128], xin[:, q, c * 128:(c + 1) * 128], self.idf[:]),
                     reads=self.xk(q) + ["idf"], writes=[("ps", b)])
            self.copy(self.evac_eng(), dst[:, c, ncols_off:ncols_off + nrows], PS[:, b, 0:nrows], [("ps", b)], [(dst_key, c)])

    def rmsnorm(self, src, src_key, C, gcol, dst, dst_key, n, sq, sq_key, dim, src_is_bf=False):
        S, PS = self.S, self.PS
        for c in range(C):
            S.op("act", lambda e, c=c: e.activation(out=sq[:, c, 0:n], in_=src[:, c, 0:n], func=AF.Square),
                 reads=[(src_key, c)], writes=[(sq_key, c)])
        b = self.nb()
        for c in range(C):
            S.op("pe", lambda e, c=c: e.matmul(PS[:, b, 0:n], lhsT=self.onesb[:], rhs=sq[:, c, 0:n], start=(c == 0), stop=(c == C - 1)),
                 reads=[(sq_key, c), "onesb"], writes=[("ps", b)])
        r = self.rs[self.ev % 2]
        rk = ("rs", self.ev % 2)
        self.ev += 1
        S.op("act", lambda e: e.activation(out=r[:, 0:n], in_=PS[:, b, 0:n], func=AF.Ln, bias=self.epsb[:, 0:1], scale=1.0 / dim),
             reads=[("ps", b), "epsb"], writes=[rk])
        S.op("act", lambda e: e.activation(out=r[:, 0:n], in_=r[:, 0:n], func=AF.Exp, scale=-0.5), reads=[rk], writes=[rk])
        for c in range(C):
            S.op("dve", lambda e, c=c: e.scalar_tensor_tensor(out=dst[:, c, 0:n], in0=src[:, c, 0:n], scalar=self.cp[:, gcol + c:gcol + c + 1],
                                                               in1=r[:, 0:n], op0=ALU.mult, op1=ALU.mult),
                 reads=[(src_key, c), rk, "cp"], writes=[(dst_key, c)])

    def linear_gen(self, wname, c0, nout, rhs, rhs_key, evac, n, KC=8):
        S, PS = self.S, self.PS
        m = 0
        while m < nout:
            nm = min(4, nout - m)
            t, key = self.wslab(wname, 0, KC, c0 + m * 128, nm * 128)
            for mm in range(nm):
                b = self.nb()
                for k in range(KC):
                    S.op("pe", lambda e, t=t, mm=mm, k=k, b=b: e.matmul(PS[:, b, 0:n], lhsT=t[:, k, mm * 128:(mm + 1) * 128], rhs=rhs[:, k, 0:n],
                                                                         start=(k == 0), stop=(k == KC - 1)),
                         reads=[key, (rhs_key, k)], writes=[("ps", b)])
                evac(m + mm, b)
                yield
            m += nm

    def linear(self, *a, **kw):
        for _ in self.linear_gen(*a, **kw):
            pass

    def interleave(self, gens, ratio=None):
        gens = list(gens)
        ratio = ratio or [1] * len(gens)
        alive = [True] * len(gens)
        while any(alive):
            for i, g in enumerate(gens):
                for _ in range(ratio[i]):
                    if alive[i]:
                        try:
                            next(g)
                        except StopIteration:
                            alive[i] = False

    def linear_bigk(self, wname, KCT, rhs, rhs_key, evac, n):
        S, PS = self.S, self.PS
        for hf in range(2):
            banks = [self.nb() for _ in range(4)]
            k0 = 0
            while k0 < KCT:
                nk = min(8, KCT - k0)
                t, key = self.wslab(wname, k0, nk, hf * 512, 512)
                for mm in range(4):
                    for k in range(nk):
                        S.op("pe", lambda e, t=t, mm=mm, k=k, kk=k0 + k, b=banks[mm]: e.matmul(
                            PS[:, b, 0:n], lhsT=t[:, k, mm * 128:(mm + 1) * 128], rhs=rhs[:, kk, 0:n], start=(kk == 0), stop=(kk == KCT - 1)),
                            reads=[key, (rhs_key, k0 + k)], writes=[("ps", banks[mm])])
                k0 += nk
            for mm in range(4):
                evac(hf * 4 + mm, banks[mm])

    def resid_add(self, m, b, scale, n):
        h, PS = self.h, self.PS
        self.S.op("dve", lambda e: e.scalar_tensor_tensor(out=h[:, m, 0:n], in0=PS[:, b, 0:n], scalar=scale, in1=h[:, m, 0:n],
                                                          op0=ALU.mult, op1=ALU.add), reads=[("ps", b), ("h", m)], writes=[("h", m)])

    def ffn(self, which):
        S, PS, T = self.S, self.PS, self.T
        W1, u = self.W1, self.u
        self.rmsnorm(self.h, "h", 8, CP_G[which], u, "u", T, self.W3, "W3", D)
        wgu, wd = which + "_w_gate_up", which + "_w_down"
        j = 0
        while j < NJ:
            nj = min(4, NJ - j)
            tg, kg = self.wslab(wgu, 0, 8, j * 128, nj * 128)
            tu, ku = self.wslab(wgu, 0, 8, DFF + j * 128, nj * 128)
            for jj in range(nj):
                bg, bu = self.nb(), self.nb()
                for (t, key, b) in ((tg, kg, bg), (tu, ku, bu)):
                    for k in range(8):
                        S.op("pe", lambda e, t=t, jj=jj, k=k, b=b: e.matmul(PS[:, b, 0:T], lhsT=t[:, k, jj * 128:(jj + 1) * 128], rhs=u[:, k, :],
                                                                             start=(k == 0), stop=(k == 7)),
                             reads=[key, ("u", k)], writes=[("ps", b)])
                tfi = (j + jj) % 2
                tf = self.tf[tfi]
                S.op("act", lambda e, tf=tf, bg=bg: e.activation(out=tf[:], in_=PS[:, bg, 0:T], func=AF.Silu), reads=[("ps", bg)], writes=[("tf", tfi)])
                S.op("dve", lambda e, tf=tf, bu=bu, jx=j + jj: e.tensor_tensor(out=W1[:, jx, :], in0=tf[:], in1=PS[:, bu, 0:T], op=ALU.mult),
                     reads=[("tf", tfi), ("ps", bu)], writes=[("W1", j + jj)])
            j += nj
        self.linear_bigk(wd, NJ, W1, "W1", lambda m, b: self.resid_add(m, b, 0.5, T), T)

    def seq_setup(self, s):
        S, PS = self.S, self.PS
        S.op("pool", lambda e: e.memset(self.state[:], 0.0), writes=[("state", g) for g in range(4)])
        S.op("pool", lambda e: e.memset(self.prevb[:], 0.0), writes=[("prevb", g) for g in range(4)])
        S.op("pool", lambda e: e.memset(self.histx[:], 0.0), writes=[("histx", c) for c in range(24)])
        S.op("pool", lambda e: e.memset(self.hista[:], 0.0), writes=[("hista", c) for c in range(8)])
        self.load_fm(("mem", s), self.mem_d[s], MEM, self.h, "h")
        self.rmsnorm(self.h, "h", 8, CP_G["mem"], self.u, "u", MEM, self.W3, "W3", D)
        KT, V, u = self.KT, self.V, self.u

        def evK(m, b):
            self.copy(self.evac_eng(), KT[:, m, :], PS[:, b, 0:MEM], [("ps", b)], [("KT", m)])
        self.linear("w_kv", 0, 8, u, "u", evK, MEM)
        for hd in range(4):
            for dd in range(2):
                c = 2 * hd + dd
                S.op("act", lambda e, c=c: e.activation(out=self.W3[:, c, 0:MEM], in_=KT[:, c, :], func=AF.Square), reads=[("KT", c)], writes=[("W3", c)])
            b = self.nb()
            for dd in range(2):
                c = 2 * hd + dd
                S.op("pe", lambda e, c=c, dd=dd, b=b: e.matmul(PS[:, b, 0:MEM], lhsT=self.onesb[:], rhs=self.W3[:, c, 0:MEM], start=(dd == 0), stop=(dd == 1)),
                     reads=[("W3", c), "onesb"], writes=[("ps", b)])
            S.op("dve", lambda e, hd=hd, b=b: e.tensor_reduce(out=self.kmax2[:, hd:hd + 1], in_=PS[:, b, 0:MEM], axis=mybir.AxisListType.X, op=ALU.max),
                 reads=[("ps", b)], writes=[("kmax2", hd)])
        for hf in range(2):
            t, key = self.wslab("w_kv", 0, 8, D + hf * 512, 512)
            for mc in range(2):
                b = self.nb()
                for k in range(8):
                    S.op("pe", lambda e, t=t, k=k, mc=mc, b=b: e.matmul(PS[:, b, :], lhsT=u[:, k, mc * 128:(mc + 1) * 128], rhs=t[:, k, :],
                                                                         start=(k == 0), stop=(k == 7)),
                         reads=[key, ("u", k)], writes=[("ps", b)])
                self.copy(self.evac_eng(), V[:, mc, hf * 512:(hf + 1) * 512], PS[:, b, :], [("ps", b)], [("V", mc)])

    def tile(self, s, ti):
        S, PS, T, QT = self.S, self.PS, self.T, self.QT
        h, u, W1, W2, W3, cp = self.h, self.u, self.W1, self.W2, self.W3, self.cp
        t0 = ti * T
        self.load_fm(("x", s, ti), self.x_d[s, t0:t0 + T, :], T, h, "h")
        self.dump("h0", h[:], [("h", c) for c in range(8)], [128, 8, T])
        self.ffn("ffn1")
        self.dump("h1", h[:], [("h", c) for c in range(8)], [128, 8, T])
        self.rmsnorm(h, "h", 8, CP_G["mix"], u, "u", T, W3, "W3", D)
        tdt, kdt = self.wslab("w_in", 0, 8, C_DT, 32)
        bdt = self.nb()
        for q in range(QT):
            for k in range(8):
                S.op("pe", lambda e, q=q, k=k, bdt=bdt, tdt=tdt: e.matmul(PS[:, bdt, q * 32:(q + 1) * 32], lhsT=u[:, k, q * 128:(q + 1) * 128], rhs=tdt[:, k, 0:32],
                                                        start=(k == 0), stop=(k == 7)), reads=[kdt, ("u", k)], writes=[("ps", bdt)])
        dtt, lat, sp1, sp2 = self.dtt, self.lat, self.sp1, self.sp2
        psv = PS[:, bdt, 0:QT * 32].rearrange("p (q h) -> p q h", h=32)
        bias_b = cp[:, CP_DTB:CP_DTB + 32].unsqueeze(1).to_broadcast([128, QT, 32])
        S.op("dve", lambda e: e.tensor_tensor(out=sp1[:], in0=psv, in1=bias_b, op=ALU.add), reads=[("ps", bdt), "cp"], writes=["sp1"])
        S.op("act", lambda e: e.activation(out=sp2[:], in_=sp1[:], func=AF.Abs), reads=["sp1"], writes=["sp2"])
        S.op("act", lambda e: e.activation(out=sp2[:], in_=sp2[:], func=AF.Exp, scale=-1.0), reads=["sp2"], writes=["sp2"])
        S.op("act", lambda e: e.activation(out=sp2[:], in_=sp2[:], func=AF.Ln, bias=self.oneb[:, 0:1]), reads=["sp2", "oneb"], writes=["sp2"])
        S.op("dve", lambda e: e.scalar_tensor_tensor(out=dtt[:], in0=sp1[:], scalar=0.0, in1=sp2[:], op0=ALU.max, op1=ALU.add),
             reads=["sp1", "sp2"], writes=["dtt"])
        S.op("dve", lambda e: e.tensor_tensor(out=lat[:], in0=dtt[:], in1=self.aneg[:].unsqueeze(1).to_broadcast([128, QT, 32]), op=ALU.mult),
             reads=["dtt", "aneg"], writes=["lat"])
        S.op("dve", lambda e: e.tensor_copy(out=self.latb[:], in_=lat[:]), reads=["lat"], writes=["latb"])
        self.dump("dtt", dtt[:], ["dtt"], [128, QT, 32])
        def evx(c, b):
            ri = c % 2
            raw, rk = self.raw[ri], ("raw", ri)
            S.op("pool", lambda e: e.tensor_copy(out=raw[:, 0:3], in_=self.histx[:, c, :]), reads=[("histx", c)], writes=[rk])
            S.op("act", lambda e: e.activation(out=raw[:, 3:3 + T], in_=PS[:, b, 0:T], func=AF.Copy), reads=[("ps", b)], writes=[rk])
            S.op("pool", lambda e: e.tensor_copy(out=self.histx[:, c, :], in_=raw[:, T:T + 3]), reads=[rk], writes=[("histx", c)])
            tf, tk = self.tf[ri], ("tf", ri)
            w = lambda k: cp[:, CP_SCW + k * 24 + c:CP_SCW + k * 24 + c + 1]
            S.op("act", lambda e: e.activation(out=tf[:], in_=PS[:, b, 0:T], func=AF.Identity, scale=w(3), bias=cp[:, CP_SCB + c:CP_SCB + c + 1]),
                 reads=[("ps", b), "cp"], writes=[tk])
            if self.pending_silu is not None:
                self.pending_silu()
            for k in range(3):
                S.op("dve", lambda e, k=k: e.scalar_tensor_tensor(out=tf[:], in0=raw[:, k:k + T], scalar=w(k), in1=tf[:], op0=ALU.mult, op1=ALU.add),
                     reads=[rk, tk, "cp"], writes=[tk])
            self.pending_silu = lambda: S.op("act", lambda e: e.activation(out=W2[:, c, 0:T], in_=tf[:], func=AF.Silu), reads=[tk], writes=[("W2", c)])
        def evz(m, b):
            S.op("act", lambda e: e.activation(out=W1[:, m, :], in_=PS[:, b, 0:T], func=AF.Silu), reads=[("ps", b)], writes=[("W1", m)])
        self.pending_silu = None
        self.interleave([self.linear_gen("w_in", C_XBC, 24, u, "u", evx, T), self.linear_gen("w_in", C_Z, 16, u, "u", evz, T)], ratio=[3, 1])
        self.pending_silu()
        self.dump("xbc", W2[:, :, 0:T], [("W2", c) for c in range(24)], [128, 24, T])
        def ev_ac(m, b):
            S.op("act", lambda e: e.activation(out=self.mg[:, m % 4, :], in_=PS[:, b, 0:T], func=AF.Copy), reads=[("ps", b)], writes=[("mg", m % 4)])
        def ev_av(m, b):
            S.op("pool", lambda e: e.tensor_copy(out=self.cv[:, m, 0:2], in_=self.hista[:, m, :]), reads=[("hista", m)], writes=self.cvk(m))
            S.op("dve", lambda e: e.tensor_tensor(out=self.cv[:, m, 2:2 + T], in0=self.mg[:, m % 4, :], in1=PS[:, b, 0:T], op=ALU.mult),
                 reads=[("mg", m % 4), ("ps", b)], writes=self.cvk(m))
        def agen():
            for hf in range(2):
                yield from self.linear_gen("w_in", C_AC + hf * 512, 4, u, "u", lambda m, b, hf=hf: ev_ac(m + hf * 4, b), T)
                yield from self.linear_gen("w_in", C_AV + hf * 512, 4, u, "u", lambda m, b, hf=hf: ev_av(m + hf * 4, b), T)
        self.agen = agen()
        self.ssd_tile()
        self.dump("yg", W3[:], [("W3", c) for c in range(16)], [128, 16, T])
        cv = self.cv
        def ev_ab(m, b):
            tfi = m % 2
            tf, tk = self.tf[tfi], ("tf", tfi)
            w = lambda k: cp[:, CP_CAW + k * 8 + m:CP_CAW + k * 8 + m + 1]
            S.op("dve", lambda e: e.tensor_scalar(out=tf[:], in0=cv[:, m, 0:T], scalar1=w(0), scalar2=None, op0=ALU.mult), reads=self.cvk(m) + ["cp"], writes=[tk])
            for k in (1, 2):
                S.op("dve", lambda e, k=k: e.scalar_tensor_tensor(out=tf[:], in0=cv[:, m, k:k + T], scalar=w(k), in1=tf[:], op0=ALU.mult, op1=ALU.add),
                     reads=self.cvk(m) + [tk, "cp"], writes=[tk])
            S.op("dve", lambda e: e.tensor_tensor(out=W2[:, m, 0:T], in0=tf[:], in1=PS[:, b, 0:T], op=ALU.mult),
                 reads=[tk, ("ps", b)], writes=[("W2", m)])
            S.op("pool", lambda e: e.tensor_copy(out=self.hista[:, m, :], in_=cv[:, m, T:T + 2]), reads=self.cvk(m), writes=[("hista", m)])
        self.group_norm(1)
        gab = self.linear_gen("w_in", C_AB, 8, u, "u", ev_ab, T)
        for _ in range(4):
            next(gab)
        self.group_norm(2)
        for _ in gab:
            pass
        self.group_norm(3)
        for hf in range(2):
            ta, tb = self.tab
            def ev_ga(m, b):
                S.op("act", lambda e: e.activation(out=ta[:, m, :], in_=PS[:, b, 0:T], func=AF.Tanh, scale=0.5), reads=[("ps", b)], writes=[("ta", m)])
            self.linear("w_in", C_GA + hf * 512, 4, u, "u", ev_ga, T)
            def ev_ya(m, b):
                S.op("dve", lambda e: e.scalar_tensor_tensor(out=self.mg[:, m, :], in0=ta[:, m, :], scalar=1.0, in1=PS[:, b, 0:T], op0=ALU.add, op1=ALU.mult),
                     reads=[("ta", m), ("ps", b)], writes=[("mg", m)])
            t, key = self.wslab("w_out_a", 0, 8, hf * 512, 512)
            for mm in range(4):
                b = self.nb()
                for k in range(8):
                    S.op("pe", lambda e, t=t, mm=mm, k=k, b=b: e.matmul(PS[:, b, 0:T], lhsT=t[:, k, mm * 128:(mm + 1) * 128], rhs=W2[:, k, 0:T],
                                                                         start=(k == 0), stop=(k == 7)), reads=[key, ("W2", k)], writes=[("ps", b)])
                ev_ya(mm, b)
            def ev_gb(m, b):
                S.op("act", lambda e: e.activation(out=tb[:, m, :], in_=PS[:, b, 0:T], func=AF.Tanh, scale=0.5), reads=[("ps", b)], writes=[("ta", m)])
            self.linear("w_in", C_GB + hf * 512, 4, u, "u", ev_gb, T)
            banks = [self.nb() for _ in range(4)]
            for ks in range(2):
                t, key = self.wslab("w_out_ssm", ks * 8, 8, hf * 512, 512)
                for mm in range(4):
                    for k in range(8):
                        kk = ks * 8 + k
                        S.op("pe", lambda e, t=t, mm=mm, k=k, kk=kk, b=banks[mm]: e.matmul(PS[:, b, 0:T], lhsT=t[:, k, mm * 128:(mm + 1) * 128], rhs=W3[:, kk, :],
                                                                                              start=(kk == 0), stop=(kk == 15)),
                             reads=[key, ("W3", kk)], writes=[("ps", banks[mm])])
            for mm in range(4):
                b = banks[mm]
                tfi = mm % 2
                tf, tk = self.tf[tfi], ("tf", tfi)
                S.op("dve", lambda e, mm=mm, b=b, tf=tf: e.scalar_tensor_tensor(out=tf[:], in0=tb[:, mm, :], scalar=1.0, in1=PS[:, b, 0:T], op0=ALU.add, op1=ALU.mult),
                     reads=[("ta", mm), ("ps", b)], writes=[tk])
                S.op("dve", lambda e, mm=mm, tf=tf, hf=hf: e.tensor_tensor(out=W1[:, hf * 4 + mm, :], in0=tf[:], in1=self.mg[:, mm, :], op=ALU.add),
                     reads=[tk, ("mg", mm)], writes=[("W1", hf * 4 + mm)])
        self.linear("w_mix_out", 0, 8, W1, "W1", lambda m, b: self.resid_add(m, b, 0.5, T), T)
        self.dump("h2", h[:], [("h", c) for c in range(8)], [128, 8, T])
        self.rmsnorm(h, "h", 8, CP_G["xattn"], u, "u", T, W3, "W3", D)
        qb = W1
        def ev_q(m, b):
            self.copy(self.evac_eng(), qb[:, m, :], PS[:, b, 0:T], [("ps", b)], [("W1", m)], scale=1.0 / 16.0)
        self.linear("w_q", 0, 8, u, "u", ev_q, T)
        KT, V = self.KT, self.V
        negc = self.negc
        for hd in range(4):
            for dd in range(2):
                c = 2 * hd + dd
                S.op("act", lambda e, c=c: e.activation(out=W3[:, c, :], in_=qb[:, c, :], func=AF.Square), reads=[("W1", c)], writes=[("W3", c)])
            bq = self.nb()
            for dd in range(2):
                c = 2 * hd + dd
                S.op("pe", lambda e, c=c, dd=dd, bq=bq: e.matmul(PS[:, bq, 0:T], lhsT=self.onesb[:], rhs=W3[:, c, :], start=(dd == 0), stop=(dd == 1)),
                     reads=[("W3", c), "onesb"], writes=[("ps", bq)])
            S.op("dve", lambda e, hd=hd, bq=bq: e.tensor_reduce(out=negc[:, hd:hd + 1], in_=PS[:, bq, 0:T], axis=mybir.AxisListType.X, op=ALU.max),
                 reads=[("ps", bq)], writes=[("negc", hd)])
        S.op("dve", lambda e: e.tensor_tensor(out=negc[:], in0=negc[:], in1=self.kmax2[:], op=ALU.mult),
             reads=[("negc", i) for i in range(4)] + [("kmax2", i) for i in range(4)], writes=[("negc", i) for i in range(4)])
        S.op("act", lambda e: e.activation(out=negc[:], in_=negc[:], func=AF.Ln, bias=self.epsb[:, 0:1]), reads=[("negc", i) for i in range(4)] + ["epsb"], writes=[("negc", i) for i in range(4)])
        S.op("act", lambda e: e.activation(out=negc[:], in_=negc[:], func=AF.Exp, scale=0.5), reads=[("negc", i) for i in range(4)], writes=[("negc", i) for i in range(4)])
        S.op("dve", lambda e: e.tensor_scalar(out=negc[:], in0=negc[:], scalar1=60.0, scalar2=-1.0, op0=ALU.min, op1=ALU.mult),
             reads=[("negc", i) for i in range(4)], writes=[("negc", i) for i in range(4)])
        def att_s1(hd):
            ET, ek = self.ET[hd % 2], "ET%d" % (hd % 2)
            for mc in range(2):
                b = self.nb()
                for dd in range(2):
                    S.op("pe", lambda e, mc=mc, dd=dd, b=b: e.matmul(PS[:, b, 0:T], lhsT=KT[:, 2 * hd + dd, mc * 128:(mc + 1) * 128], rhs=qb[:, 2 * hd + dd, :],
                                                                      start=(dd == 0), stop=(dd == 1)),
                         reads=[("KT", 2 * hd + dd), ("W1", 2 * hd + dd)], writes=[("ps", b)])
                S.op("act", lambda e, mc=mc, b=b: e.activation(out=ET[:, mc, :], in_=PS[:, b, 0:T], func=AF.Exp, bias=negc[:, hd:hd + 1]),
                     reads=[("ps", b), ("negc", hd)], writes=[(ek, mc)])

        def att_s2(hd):
            ET, ek = self.ET[hd % 2], "ET%d" % (hd % 2)
            bd = self.nb()
            for mc in range(2):
                S.op("pe", lambda e, mc=mc: e.matmul(PS[:, bd, 0:T], lhsT=self.onesb[:], rhs=ET[:, mc, :], start=(mc == 0), stop=(mc == 1)),
                     reads=[(ek, mc), "onesb"], writes=[("ps", bd)])
            r, rk = self.rs[hd % 2], ("rs", hd % 2)
            S.op("dve", lambda e: e.reciprocal(out=r[:], in_=PS[:, bd, 0:T]), reads=[("ps", bd)], writes=[rk])
            for dd in range(2):
                b = self.nb()
                for mc in range(2):
                    S.op("pe", lambda e, mc=mc, dd=dd, b=b: e.matmul(PS[:, b, 0:T], lhsT=V[:, mc, (2 * hd + dd) * 128:(2 * hd + dd + 1) * 128], rhs=ET[:, mc, :],
                                                                      start=(mc == 0), stop=(mc == 1)),
                         reads=[("V", mc), (ek, mc)], writes=[("ps", b)])
                S.op("dve", lambda e, dd=dd, b=b: e.tensor_tensor(out=W1[:, 8 + 2 * hd + dd, :], in0=PS[:, b, 0:T], in1=r[:], op=ALU.mult),
                     reads=[("ps", b), rk], writes=[("W1", 8 + 2 * hd + dd)])

        att_s1(0)
        for hd in range(4):
            if hd + 1 < 4:
                att_s1(hd + 1)
            att_s2(hd)
        OT = W1[:, 8:16, :]
        t_keys = None
        for hf in range(2):
            t, key = self.wslab("w_o_x", 0, 8, hf * 512, 512)
            for mm in range(4):
                b = self.nb()
                for k in range(8):
                    S.op("pe", lambda e, t=t, mm=mm, k=k, b=b: e.matmul(PS[:, b, 0:T], lhsT=t[:, k, mm * 128:(mm + 1) * 128], rhs=W1[:, 8 + k, :],
                                                                         start=(k == 0), stop=(k == 7)), reads=[key, ("W1", 8 + k)], writes=[("ps", b)])
                self.resid_add(hf * 4 + mm, b, 1.0, T)
        self.dump("h3", h[:], [("h", c) for c in range(8)], [128, 8, T])
        self.ffn("ffn2")
        if ti + 1 < self.NT:
            self.issue_load(("x", s, ti + 1), self.x_d[s, t0 + T:t0 + 2 * T, :], T)
        elif s + 1 < self.NSEQ:
            self.issue_load(("mem", s + 1), self.mem_d[s + 1], MEM)
        io = self.io
        junk = self.tf[0][:].bitcast(BF16)
        ssq, ssum = self.ssq, self.ssum
        for q in range(QT):
            banks = [self.nb(), self.nb()]
            for hf in range(2):
                b = banks[hf]
                for cc in range(4):
                    c = hf * 4 + cc
                    S.op("pe", lambda e, c=c, cc=cc, q=q, b=b: e.transpose(PS[:, b, cc * 128:(cc + 1) * 128], h[:, c, q * 128:(q + 1) * 128], self.idf[:]),
                         reads=[("h", c), "idf"], writes=[("ps", b)])
            for hf in range(2):
                S.op("act", lambda e, q=q, hf=hf, b=banks[hf]: e.activation(out=junk[:, 0:512], in_=PS[:, b, :], func=AF.Square, accum_out=ssq[:, 2 * q + hf:2 * q + hf + 1]),
                     reads=[("ps", banks[hf])], writes=[("tf", 0), ("ssq", 2 * q + hf)])
            S.op("dve", lambda e, q=q: e.tensor_tensor(out=ssum[:, q:q + 1], in0=ssq[:, 2 * q:2 * q + 1], in1=ssq[:, 2 * q + 1:2 * q + 2], op=ALU.add),
                 reads=[("ssq", 2 * q), ("ssq", 2 * q + 1)], writes=[("ssum", q)])
            S.op("act", lambda e, q=q: e.activation(out=ssum[:, q:q + 1], in_=ssum[:, q:q + 1], func=AF.Ln, bias=self.epsb[:, 0:1], scale=1.0 / D),
                 reads=[("ssum", q), "epsb"], writes=[("ssum", q)])
            S.op("act", lambda e, q=q: e.activation(out=ssum[:, q:q + 1], in_=ssum[:, q:q + 1], func=AF.Exp, scale=-0.5), reads=[("ssum", q)], writes=[("ssum", q)])
            for hf in range(2):
                S.op("dve", lambda e, q=q, hf=hf, b=banks[hf]: e.scalar_tensor_tensor(
                    out=io[:, q, hf * 512:(hf + 1) * 512], in0=PS[:, b, :], scalar=ssum[:, q:q + 1], in1=cp[:, CP_GFB + hf * 512:CP_GFB + (hf + 1) * 512],
                    op0=ALU.mult, op1=ALU.mult), reads=[("ps", banks[hf]), ("ssum", q), "cp"], writes=self.iok(q))
        dst = self.y_d[s, t0:t0 + T, :].rearrange("(q p) d -> p q d", p=128)
        S.dma("pool", lambda e: e.dma_start(out=dst, in_=io[:, 0:QT, :]), reads=[k for q in range(QT) for k in self.iok(q)], writes=["y_out"], key="yout")

    def ssd_tile(self):
        QT = self.QT
        items = [(q, g) for q in range(QT) for g in range(4)]
        asteps = [16 // len(items) + (1 if i < 16 % len(items) else 0) for i in range(len(items))]
        self.ssd_R(*items[0])
        self.ssd_R(*items[1])
        for i, (q, g) in enumerate(items):
            if g == 0:
                self.ssd_pre(q)
            self.ssd_A(q, g)
            for _ in range(asteps[i]):
                next(self.agen, None)
            if i + 2 < len(items):
                self.ssd_R(*items[i + 2])
            if g >= 1:
                self.ssd_B(q, g - 1)
            if g == 3:
                if q == QT - 1:
                    for _ in self.agen:
                        pass
                    self.group_norm(0)
                self.ssd_B(q, 3)

    def group_norm(self, g):
        S, PS, T = self.S, self.PS, self.T
        W2, W3, cp = self.W2, self.W3, self.cp
        S.op("act", lambda e: e.activation(out=W2[:, 4 * g:4 * g + 4, 0:T], in_=W3[:, 4 * g:4 * g + 4, :], func=AF.Square),
             reads=[("W3", 4 * g + i) for i in range(4)], writes=[("W2", 4 * g + i) for i in range(4)])
        b = self.nb()
        for i in range(4):
            S.op("pe", lambda e, i=i: e.matmul(PS[:, b, 0:T], lhsT=self.onesb[:], rhs=W2[:, 4 * g + i, 0:T], start=(i == 0), stop=(i == 3)),
                 reads=[("W2", 4 * g + i), "onesb"], writes=[("ps", b)])
        r, rk = self.rs[g % 2], ("rs", g % 2)
        S.op("act", lambda e: e.activation(out=r[:], in_=PS[:, b, 0:T], func=AF.Ln, bias=self.epsb[:, 0:1], scale=1.0 / 512),
             reads=[("ps", b), "epsb"], writes=[rk])
        S.op("act", lambda e: e.activation(out=r[:], in_=r[:], func=AF.Exp, scale=-0.5), reads=[rk], writes=[rk])
        for i in range(4):
            c = 4 * g + i
            S.op("dve", lambda e, c=c: e.scalar_tensor_tensor(out=W3[:, c, :], in0=W3[:, c, :], scalar=cp[:, CP_GN + c:CP_GN + c + 1], in1=r[:],
                                                              op0=ALU.mult, op1=ALU.mult), reads=[("W3", c), rk, "cp"], writes=[("W3", c)])

    def ssd_R(self, q, g):
        gi = g % 2
        R = self.R[gi]
        if g % 2 == 0:
            self.S.op("pool", lambda e: e.tensor_tensor(out=R[:], in0=self.lat[:, q, g * 8:(g + 1) * 8].unsqueeze(2).to_broadcast([128, 8, 128]),
                                                        in1=self.Uf[:].unsqueeze(1).to_broadcast([128, 8, 128]), op=ALU.mult),
                      reads=["lat", "Uf"], writes=[("R", gi)])
        else:
            for h in range(8):
                self.S.op("act", lambda e, h=h: e.activation(out=R[:, h, :], in_=self.Uf[:], func=AF.Copy, scale=self.lat[:, q, g * 8 + h:g * 8 + h + 1]),
                          reads=["lat", "Uf"], writes=[("R", gi)])

    def ssd_pre(self, q):
        S, PS = self.S, self.PS
        W2 = self.W2
        c0 = q * 128
        lat, dtt = self.lat, self.dtt
        ex, exk = self.exs[q % 2], ("ex", q % 2)
        b = self.nb()
        for i, L in enumerate((self.Uf, self.Lf, self.onesf)):
            S.op("pe", lambda e, i=i, L=L, b=b: e.matmul(PS[:, b, i * 32:(i + 1) * 32], lhsT=L[:], rhs=lat[:, q, :], start=True, stop=True),
                 reads=["lat", "Uf", "Lf", "onesf"], writes=[("ps", b)])
        S.op("act", lambda e: e.activation(out=ex[:].rearrange("p a h -> p (a h)"), in_=PS[:, b, 0:96], func=AF.Exp), reads=[("ps", b)], writes=[exk])
        xdt, xdte, Btm, CBm = self.xdt, self.xdte, self.Btm, self.CBm
        dd, ddk = self.dds[q % 2], ("dd", q % 2)
        S.op("dve", lambda e: e.tensor_tensor(out=dd[:], in0=dtt[:, q, :], in1=ex[:, 1, :], op=ALU.mult), reads=["dtt", exk], writes=[ddk])
        for half in range(2):
            bb = self.nb()
            pbf = PS[:, bb, :].bitcast(BF16)
            for cc in range(8):
                c = half * 8 + cc
                S.op("pe", lambda e, c=c, cc=cc, pbf=pbf: e.transpose(pbf[:, cc * 128:(cc + 1) * 128], W2[:, c, c0:c0 + 128], self.idb[:]),
                     reads=[("W2", c), "idb"], writes=[("ps", bb)])
            S.op("dve", lambda e, half=half, pbf=pbf: e.tensor_tensor(
                out=xdt[:, half * 16:(half + 1) * 16, :], in0=pbf[:, 0:1024].rearrange("p (h d) -> p h d", d=64),
                in1=dtt[:, q, half * 16:(half + 1) * 16].unsqueeze(2).to_broadcast([128, 16, 64]), op=ALU.mult),
                reads=[("ps", bb), "dtt"], writes=[("xdt", half)])
            S.op("dve", lambda e, half=half, pbf=pbf: e.tensor_tensor(
                out=xdte[:, half * 16:(half + 1) * 16, :], in0=pbf[:, 0:1024].rearrange("p (h d) -> p h d", d=64),
                in1=dd[:, half * 16:(half + 1) * 16].unsqueeze(2).to_broadcast([128, 16, 64]), op=ALU.mult),
                reads=[("ps", bb), ddk], writes=[("xdte", half)])
        bb = self.nb()
        pbf2 = PS[:, bb, :].bitcast(BF16)
        for g in range(4):
            S.op("pe", lambda e, g=g: e.transpose(pbf2[:, g * 128:(g + 1) * 128], W2[:, 16 + g, c0:c0 + 128], self.idb[:]),
                 reads=[("W2", 16 + g), "idb"], writes=[("ps", bb)])
        S.op("act", lambda e: e.activation(out=Btm[:].rearrange("p g n -> p (g n)"), in_=pbf2[:, 0:512], func=AF.Copy), reads=[("ps", bb)], writes=["Btm"])
        bc = self.nb()
        for g in range(4):
            S.op("pe", lambda e, g=g: e.matmul(PS[:, bc, g * 128:(g + 1) * 128], lhsT=W2[:, 16 + g, c0:c0 + 128], rhs=W2[:, 20 + g, c0:c0 + 128], start=True, stop=True),
                 reads=[("W2", 16 + g), ("W2", 20 + g)], writes=[("ps", bc)])
        S.op("dve", lambda e: e.tensor_tensor(out=CBm[:], in0=PS[:, bc, :].rearrange("p (g l) -> p g l", l=128),
                                              in1=self.Uf[:].unsqueeze(1).to_broadcast([128, 4, 128]), op=ALU.mult),
             reads=[("ps", bc), "Uf"], writes=["CBm"])

    def ssd_A(self, q, g):
        S, PS = self.S, self.PS
        W2 = self.W2
        c0 = q * 128
        lat = self.lat
        ex, exk = self.exs[q % 2], ("ex", q % 2)
        gi = g % 2
        R, E, MT, yo = self.R[gi], self.E[gi], self.MT[gi], self.yo[gi]
        Rk, Ek, Mk, yk = ("R", gi), ("E", gi), ("MT", gi), ("yo", gi)
        bo = self.nb()
        S.op("pe", lambda e: e.matmul(PS[:, bo, :], lhsT=W2[:, 20 + g, c0:c0 + 128], rhs=self.prevb[:, g * 8:(g + 1) * 8, :].rearrange("p h d -> p (h d)"),
                                      start=True, stop=True), reads=[("W2", 20 + g), ("prevb", g)], writes=[("ps", bo)])
        S.op("dve", lambda e: e.tensor_tensor(out=yo[:], in0=PS[:, bo, :].rearrange("p (h d) -> p h d", d=64),
                                              in1=ex[:, 0, g * 8:(g + 1) * 8].unsqueeze(2).to_broadcast([128, 8, 64]), op=ALU.mult),
             reads=[("ps", bo), exk], writes=[yk])
        for hh2 in range(2):
            bs = self.nb()
            S.op("pe", lambda e, hh2=hh2, bs=bs: e.matmul(PS[:, bs, :], lhsT=self.Lb[:], rhs=R[:, hh2 * 4:(hh2 + 1) * 4, :].rearrange("p h l -> p (h l)"),
                                                          start=True, stop=True), reads=[Rk, "Lb"], writes=[("ps", bs)])
            S.op("act", lambda e, hh2=hh2, bs=bs: e.activation(out=E[:, hh2 * 4:(hh2 + 1) * 4, :].rearrange("p h l -> p (h l)"), in_=PS[:, bs, :], func=AF.Exp),
                 reads=[("ps", bs)], writes=[Ek])
        S.op("dve", lambda e: e.tensor_tensor(out=MT[:], in0=E[:], in1=self.CBm[:, g, :].unsqueeze(1).to_broadcast([128, 8, 128]), op=ALU.mult),
             reads=[Ek, "CBm"], writes=[Mk])

    def ssd_B(self, q, g):
        S, PS = self.S, self.PS
        W1, W2, W3 = self.W1, self.W2, self.W3
        c0 = q * 128
        ex, exk = self.exs[q % 2], ("ex", q % 2)
        gi = g % 2
        MT, yo = self.MT[gi], self.yo[gi]
        Mk, yk = ("MT", gi), ("yo", gi)
        xdt, xdte, Btm = self.xdt, self.xdte, self.Btm
        by = self.nb()
        for pr in range(4):
            cch = g * 4 + pr
            cols = slice(pr * 128, (pr + 1) * 128)
            S.op("pe", lambda e, pr=pr, cols=cols: e.matmul(PS[:, by, cols], lhsT=yo[:, 2 * pr:2 * pr + 2, :].rearrange("p h d -> p (h d)"), rhs=self.idb[:],
                                                            start=True, stop=False), reads=[yk, "idb"], writes=[("ps", by)])
            S.op("pe", lambda e, cch=cch, cols=cols: e.matmul(PS[:, by, cols], lhsT=self.diagD[:, cch, :], rhs=W2[:, cch, c0:c0 + 128], start=False, stop=False),
                 reads=["diagD", ("W2", cch)], writes=[("ps", by)])
            for hx in range(2):
                hh = 2 * pr + hx
                hglob = g * 8 + hh
                S.op("pe", lambda e, hx=hx, hh=hh, hglob=hglob, cols=cols: e.matmul(
                    PS[hx * 64:(hx + 1) * 64, by, cols], lhsT=xdt[:, hglob, :], rhs=MT[:, hh, :], start=False, stop=True),
                    reads=[("xdt", hglob // 16), Mk], writes=[("ps", by)])
        S.op("dve", lambda e: e.tensor_tensor(out=W3[:, 4 * g:4 * g + 4, c0:c0 + 128], in0=PS[:, by, :].rearrange("p (c l) -> p c l", l=128),
                                              in1=W1[:, 4 * g:4 * g + 4, c0:c0 + 128], op=ALU.mult),
             reads=[("ps", by)] + [("W1", 4 * g + i) for i in range(4)], writes=[("W3", 4 * g + i) for i in range(4)])
        bst = self.nb()
        S.op("pe", lambda e: e.matmul(PS[:, bst, :], lhsT=Btm[:, g, :], rhs=xdte[:, g * 8:(g + 1) * 8, :].rearrange("p h d -> p (h d)"), start=True, stop=True),
             reads=["Btm", ("xdte", g // 2)], writes=[("ps", bst)])
        stv = self.state[:, g * 8:(g + 1) * 8, :]
        stk = ("state", g)
        S.op("dve", lambda e: e.tensor_tensor(out=stv, in0=stv, in1=ex[:, 2, g * 8:(g + 1) * 8].unsqueeze(2).to_broadcast([128, 8, 64]), op=ALU.mult),
             reads=[stk, exk], writes=[stk])
        S.op("dve", lambda e: e.tensor_tensor(out=stv, in0=stv, in1=PS[:, bst, :].rearrange("p (h d) -> p h d", d=64), op=ALU.add),
             reads=[stk, ("ps", bst)], writes=[stk])
        S.op("pool", lambda e: e.tensor_copy(out=self.prevb[:, g * 8:(g + 1) * 8, :], in_=stv), reads=[stk], writes=[("prevb", g)])


def _build_once(NSEQ, NT, T, dbg, cast_order, dry):
    nc = bass.Bass("TRN2", target_bir_lowering=False)
    st = ExitStack()
    B = Builder(nc, st, NSEQ, NT, T, dbg)
    B.cast_order, B.dry = cast_order, dry
    B.epsb = B.sb("epsb", [128, 1], F32)
    B.oneb = B.sb("oneb", [128, 1], F32)
    B.S.op("pool", lambda e: e.memset(B.epsb[:], EPS), writes=["epsb"])
    B.S.op("pool", lambda e: e.memset(B.oneb[:], 1.0), writes=["oneb"])
    B.build()
    st.close()
    return nc, B


def build_program(NSEQ=4, NT=4, T=512, dbg=()):
    _, B0 = _build_once(1, 1, T, (), None, True)
    return _build_once(NSEQ, NT, T, dbg, list(B0.slab_use), False)


def pack_consts(inp):
    cp = np.zeros((128, NCP), np.float32)
    def cols(v):
        v = np.asarray(v, np.float32).reshape(-1, 128)
        return v.T
    for nm, key in (("ffn1", "ffn1_norm"), ("mix", "mix_norm"), ("xattn", "xattn_norm"), ("mem", "mem_norm"), ("ffn2", "ffn2_norm")):
        cp[:, CP_G[nm]:CP_G[nm] + 8] = cols(inp[key][0])
    cp[:, CP_G["final"]:CP_G["final"] + 8] = cols(inp["final_norm"])
    for k in range(3):
        cp[:, CP_CAW + k * 8:CP_CAW + (k + 1) * 8] = cols(inp["conv_a_w"][0, k])
    for k in range(4):
        cp[:, CP_SCW + k * 24:CP_SCW + (k + 1) * 24] = cols(inp["ssm_conv_w"][0, k])
    cp[:, CP_SCB:CP_SCB + 24] = cols(inp["ssm_conv_b"][0])
    cp[:, CP_GN:CP_GN + 16] = cols(inp["ssm_norm"][0])
    cp[:, CP_D:CP_D + 16] = cols(np.repeat(np.asarray(inp["ssm_d"][0], np.float32), 64))
    cp[:, CP_DTB:CP_DTB + 32] = np.broadcast_to(np.asarray(inp["ssm_dt_bias"][0], np.float32), (128, 32))
    cp[:, CP_ALOG:CP_ALOG + 32] = np.broadcast_to(np.asarray(inp["ssm_a_log"][0], np.float32), (128, 32))
    cp[:, CP_GFB:CP_GFB + 1024] = np.broadcast_to(np.asarray(inp["final_norm"], np.float32), (128, 1024))
    return cp


_CACHE = {}


def kernel(**inputs):
    inp = {k: np.asarray(v) for k, v in inputs.items()}
    NCORES, NSEQ, T = 8, 4, 512
    NT = SEQ // T
    if "nc" not in _CACHE:
        _CACHE["nc"] = build_program(NSEQ, NT, T)[0]
    nc = _CACHE["nc"]
    cp = pack_consts(inp)
    ws = {n: np.ascontiguousarray(inp[n][0], dtype=np.float32) for n, _, _ in BIGW}
    x = np.ascontiguousarray(inp["x"], dtype=np.float32)
    mem = np.ascontiguousarray(inp["mem"], dtype=np.float32)
    in_maps = []
    for c in range(NCORES):
        m = {"x": x[c * NSEQ:(c + 1) * NSEQ], "mem": mem[c * NSEQ:(c + 1) * NSEQ], "cpk": cp}
        m.update(ws)
        in_maps.append(m)
    res = run_bass_kernel_spmd(nc, in_maps, core_ids=list(range(NCORES)))
    out = np.concatenate([np.asarray(r["y"]) for r in res.results], axis=0)
    return out.astype(np.float32, copy=False)
```
